# Optimizing a Trainium2 kernel written in Bass

```python
import math
import jax, jax.numpy as jnp
from jax import lax
import numpy as np

D_MODEL = 2048
BATCH = 4
SEQ = 4096
DEPTH = 2

CHUNK = 64
Q_BLOCK = 128
N_MIXERS = 2
N_MOD = 9
D_FF = 5632
EPS = 1e-5

DA_HEADS = 8
DA_HEAD_DIM = D_MODEL // (2 * DA_HEADS)
DA_V_DIM = 2 * DA_HEAD_DIM
DA_QKV_WIDTH = 2 * 2 * DA_HEADS * DA_HEAD_DIM + DA_HEADS * DA_V_DIM

RET_HEADS = 8
RET_QK_DIM = D_MODEL // RET_HEADS
RET_V_DIM = 2 * RET_QK_DIM
RET_V_WIDTH = RET_HEADS * RET_V_DIM
RET_PROJ_WIDTH = 2 * RET_HEADS * RET_QK_DIM + 2 * RET_V_WIDTH

N_DA_LAYERS = (DEPTH + 1) // 2
N_RET_LAYERS = DEPTH // 2

kernel_name = "hybrid_diffattn_retention_macaron_block"

F32 = jnp.float32


def rms_norm(x, g):
    xf = x.astype(F32)
    y = xf * lax.rsqrt(jnp.mean(xf * xf, axis=-1, keepdims=True) + EPS)
    return (y * g.astype(F32)).astype(x.dtype)


def head_layer_norm(o, g):
    of = o.astype(F32)
    mu = jnp.mean(of, axis=-1, keepdims=True)
    d = of - mu
    y = d * lax.rsqrt(jnp.mean(d * d, axis=-1, keepdims=True) + EPS)
    return (y * g.astype(F32)).astype(o.dtype)


def modulate(h, shift, scale):
    return h * (1.0 + scale[:, None, :]) + shift[:, None, :]


def swiglu(h, w_in, w_out):
    a, b = jnp.split(h @ w_in, 2, axis=-1)
    return (jax.nn.silu(a) * b) @ w_out


def alibi_slopes(n_heads):
    return jnp.asarray([2.0 ** (-8.0 * (h + 1) / n_heads) for h in range(n_heads)], dtype=F32)


def diff_attention(h, w_qkv, lam, subln_g, w_o, lambda_init):
    B, S, _ = h.shape
    H, d = DA_HEADS, DA_HEAD_DIM
    q, k, v = jnp.split(h @ w_qkv, [2 * H * d, 4 * H * d], axis=-1)
    q = q.reshape(B, S, H, 2, d) * (d ** -0.5)
    k = k.reshape(B, S, H, 2, d)
    v = v.reshape(B, S, H, DA_V_DIM)
    lf = lam.astype(F32)
    lam_val = jnp.exp(jnp.sum(lf[0] * lf[1])) - jnp.exp(jnp.sum(lf[2] * lf[3])) + lambda_init
    slopes = alibi_slopes(H)
    pos = jnp.arange(S)
    key_chunk = pos // CHUNK
    nb = S // Q_BLOCK
    qb = q.reshape(B, nb, Q_BLOCK, H, 2, d).transpose(1, 0, 2, 3, 4, 5)

    def block(args):
        q_blk, b_idx = args
        t = b_idx * Q_BLOCK + jnp.arange(Q_BLOCK)
        s = jnp.einsum("bqhnd,bkhnd->bhnqk", q_blk, k).astype(F32)
        dist = jnp.abs(t[:, None] - pos[None, :]).astype(F32)
        s = s - slopes[:, None, None, None] * dist[None, None]
        allowed = key_chunk[None, :] <= (t // CHUNK)[:, None]
        s = jnp.where(allowed, s, -jnp.inf)
        p = jax.nn.softmax(s, axis=-1)
        a = p[:, :, 0] - lam_val * p[:, :, 1]
        return jnp.einsum("bhqk,bkhe->bqhe", a.astype(v.dtype), v)

    o = lax.map(block, (qb, jnp.arange(nb)))
    o = o.transpose(1, 0, 2, 3, 4).reshape(B, S, H, DA_V_DIM)
    o = rms_norm(o, subln_g) * (1.0 - lambda_init)
    return o.reshape(B, S, H * DA_V_DIM) @ w_o


def retention(h, w_qkvg, gn_g, w_o):
    B, S, _ = h.shape
    H, dk, dv, C = RET_HEADS, RET_QK_DIM, RET_V_DIM, CHUNK
    nc = S // C
    q, k, v, g = jnp.split(h @ w_qkvg, [H * dk, 2 * H * dk, 2 * H * dk + H * dv], axis=-1)
    q = q.astype(F32).reshape(B, nc, C, H, dk).transpose(1, 0, 3, 2, 4)
    k = (k.astype(F32) * (dk ** -0.5)).reshape(B, nc, C, H, dk).transpose(1, 0, 3, 2, 4)
    v = v.astype(F32).reshape(B, nc, C, H, dv).transpose(1, 0, 3, 2, 4)
    log_gamma = jnp.log(1.0 - 2.0 ** (-5.0 - jnp.arange(H, dtype=F32)))
    idx = jnp.arange(C, dtype=F32)
    inner_decay = jnp.exp(log_gamma[:, None, None] * jnp.abs(idx[:, None] - idx[None, :]))
    q_decay = jnp.exp(log_gamma[:, None] * idx[None, :])[..., None]
    k_decay = jnp.exp(log_gamma[:, None] * (C - idx)[None, :])[..., None]
    chunk_decay = jnp.exp(log_gamma * C)[:, None, None]

    def step(state, inp):
        qc, kc, vc = inp
        inner = jnp.einsum("bhid,bhjd->bhij", qc, kc) * inner_decay
        o = (jnp.einsum("bhij,bhje->bhie", inner, vc)
             + jnp.einsum("bhid,bhde->bhie", qc * q_decay, state))
        state = state * chunk_decay + jnp.einsum("bhjd,bhje->bhde", kc * k_decay, vc)
        return state, o

    state0 = jnp.zeros((B, H, dk, dv), F32)
    _, o = lax.scan(step, state0, (q, k, v))
    o = o.transpose(1, 0, 3, 2, 4).reshape(B, S, H, dv).astype(h.dtype)
    o = head_layer_norm(o, gn_g).reshape(B, S, H * dv)
    return (jax.nn.silu(g) * o) @ w_o


def setup_inputs(seed: int = 0) -> dict:
    key = jax.random.key(seed)
    ks = jax.random.split(key, 20)
    D, F = D_MODEL, D_FF
    nrm = lambda k, shape, s: jax.random.normal(k, shape, F32) * s
    return {
        "x": nrm(ks[0], (BATCH, SEQ, D), 1.0),
        "c": nrm(ks[1], (BATCH, D), 1.0),
        "ada_w": nrm(ks[2], (DEPTH, D, N_MOD * D), 0.5 * D ** -0.5),
        "ada_b": nrm(ks[3], (DEPTH, N_MOD * D), 0.02),
        "norm_g": 1.0 + nrm(ks[4], (DEPTH, 3, D), 0.02),
        "ffn_w_in": nrm(ks[5], (DEPTH, 2, D, 2 * F), D ** -0.5),
        "ffn_w_out": nrm(ks[6], (DEPTH, 2, F, D), F ** -0.5),
        "da_w_qkv": nrm(ks[7], (N_DA_LAYERS, D, DA_QKV_WIDTH), D ** -0.5),
        "da_lambda": nrm(ks[8], (N_DA_LAYERS, 4, DA_HEAD_DIM), 0.1),
        "da_subln_g": 1.0 + nrm(ks[9], (N_DA_LAYERS, DA_V_DIM), 0.02),
        "da_w_o": nrm(ks[10], (N_DA_LAYERS, DA_HEADS * DA_V_DIM, D), (DA_HEADS * DA_V_DIM) ** -0.5),
        "ret_w_qkvg": nrm(ks[11], (N_RET_LAYERS, D, RET_PROJ_WIDTH), D ** -0.5),
        "ret_gn_g": 1.0 + nrm(ks[12], (N_RET_LAYERS, RET_V_DIM), 0.02),
        "ret_w_o": nrm(ks[13], (N_RET_LAYERS, RET_V_WIDTH, D), RET_V_WIDTH ** -0.5),
        "final_g": 1.0 + nrm(ks[14], (D,), 0.02),
    }


def reference(x, c, ada_w, ada_b, norm_g, ffn_w_in, ffn_w_out, da_w_qkv, da_lambda, da_subln_g,
              da_w_o, ret_w_qkvg, ret_gn_g, ret_w_o, final_g):
    cs = jax.nn.silu(c)
    for i in range(DEPTH):
        mod = cs @ ada_w[i] + ada_b[i]
        sh1, sc1, g1, sh2, sc2, g2, sh3, sc3, g3 = jnp.split(mod, N_MOD, axis=-1)
        h = modulate(rms_norm(x, norm_g[i, 0]), sh1, sc1)
        x = x + 0.5 * g1[:, None, :] * swiglu(h, ffn_w_in[i, 0], ffn_w_out[i, 0])
        h = modulate(rms_norm(x, norm_g[i, 1]), sh2, sc2)
        j = i // N_MIXERS
        if i % N_MIXERS == 0:
            lambda_init = 0.8 - 0.6 * math.exp(-0.3 * i)
            y = diff_attention(h, da_w_qkv[j], da_lambda[j], da_subln_g[j], da_w_o[j], lambda_init)
        else:
            y = retention(h, ret_w_qkvg[j], ret_gn_g[j], ret_w_o[j])
        x = x + g2[:, None, :] * y
        h = modulate(rms_norm(x, norm_g[i, 2]), sh3, sc3)
        x = x + 0.5 * g3[:, None, :] * swiglu(h, ffn_w_in[i, 1], ffn_w_out[i, 1])
    return rms_norm(x, final_g)
```

```python
import math
import contextlib
import numpy as np
import ml_dtypes
import concourse.bass as bass
import concourse.mybir as mybir
from concourse.bass_utils import run_bass_kernel_spmd

F32 = mybir.dt.float32
BF16 = mybir.dt.bfloat16
AF = mybir.ActivationFunctionType
ALU = mybir.AluOpType
AX = mybir.AxisListType

D = 2048
FF = 5632
NKT = 16
NFC = 44
T = 512
TOK = 2048
NT = TOK // T
SEQ = 4096
EPS = 1e-5
NCORES = 8
DA_H = 8
RET_H = 8
NEG = -30000.0


class Op:
    __slots__ = ("eng", "fn", "deps", "need_inc", "cum", "key", "val", "waits", "is_dma", "ndma", "unit")


class Prog:
    ENG = ("pe", "act", "dve", "pool", "sp")

    def __init__(self):
        self.ops = []
        self.by_eng = {e: [] for e in self.ENG}
        self.lastw = {}
        self.readers = {}
        self.keycnt = {}
        self.final_keys = []

    def add(self, eng, fn, reads=(), writes=(), dma_key=None, ndma=1, unit=16):
        op = Op()
        op.unit = unit
        op.eng = eng
        op.fn = fn
        op.need_inc = False
        op.is_dma = dma_key is not None
        op.ndma = ndma
        op.cum = 0
        op.waits = []
        deps = []
        seen = set()

        def push(p):
            if p is None or p is op or id(p) in seen:
                return
            if (not p.is_dma) and p.eng == eng and eng == "pe":
                return
            seen.add(id(p))
            deps.append(p)

        for r in reads:
            push(self.lastw.get(r))
        for w in writes:
            push(self.lastw.get(w))
            for rd in self.readers.get(w, ()):
                push(rd)
        for r in reads:
            self.readers.setdefault(r, []).append(op)
        for w in writes:
            self.lastw[w] = op
            self.readers[w] = []
        if op.is_dma:
            op.key = dma_key
            c = self.keycnt.get(dma_key, 0) + unit * ndma
            self.keycnt[dma_key] = c
            op.val = c
        else:
            op.key = eng
            op.val = None
        op.deps = deps
        for p in deps:
            p.need_inc = True
        self.ops.append(op)
        self.by_eng[eng].append(op)
        return op

    def barrier(self):
        lasts = []
        for e in self.ENG:
            for op in reversed(self.by_eng[e]):
                if (not op.is_dma) and op.fn is not None:
                    lasts.append(op)
                    break
        dma_state = dict(self.keycnt)
        self._barriers = getattr(self, "_barriers", [])
        for e in self.ENG:
            op = self.add(e, None)
            for p in lasts:
                if p.eng != e:
                    op.deps.append(p)
                    p.need_inc = True
            op.waits = [(k, v) for k, v in dma_state.items()]

    def finalize(self):
        for e in self.ENG:
            c = 0
            for op in self.by_eng[e]:
                if (not op.is_dma) and op.need_inc:
                    c += 1
                op.cum = c
        waited = {e: {} for e in self.ENG}
        for op in self.ops:
            wd = waited[op.eng]
            out = []
            pre = op.waits
            for (k, v) in pre:
                if wd.get(k, 0) < v:
                    wd[k] = v
                    out.append((k, v))
            for p in op.deps:
                k = p.key
                v = p.val if p.is_dma else p.cum
                if wd.get(k, 0) < v:
                    wd[k] = v
                    out.append((k, v))
            op.waits = out

    def emit(self, nc, block, stack):
        self.finalize()
        keys = list(self.ENG) + list(self.keycnt.keys())
        sems = {}
        for i, k in enumerate(keys):
            sems[k] = stack.enter_context(nc.semaphore("s%d" % i))
        engmap = {"pe": block.tensor, "act": block.scalar, "dve": block.vector,
                  "pool": block.gpsimd, "sp": block.sync}
        final = [(k, self.keycnt[k]) for k in self.final_keys if k in self.keycnt]

        for e in self.ENG:
            ops = self.by_eng[e]

            def body(eng, ops=ops, e=e):
                for op in ops:
                    for (k, v) in op.waits:
                        eng.wait_ge(sems[k], v)
                    if op.fn is None:
                        continue
                    r = op.fn(eng)
                    if op.is_dma:
                        if not isinstance(r, (list, tuple)):
                            r = [r]
                        assert len(r) == op.ndma, (len(r), op.ndma)
                        for ins in r:
                            ins.then_inc(sems[op.key], op.unit)
                    elif op.need_inc:
                        r.then_inc(sems[op.key], 1)
                if e == "sp":
                    for (k, v) in final:
                        eng.wait_ge(sems[k], v)

            engmap[e](body)


class Builder:
    def __init__(self, stage_set):
        self.stages = stage_set
        self.nc = bass.Bass("TRN2", target_bir_lowering=False)
        self.P = Prog()
        self.dram = {}
        self.off = 0

    def din(self, name, shape, dt=F32):
        t = self.nc.dram_tensor(name, list(shape), dt, kind="ExternalInput")
        self.dram[name] = t
        return t

    def dout(self, name, shape, dt=F32):
        t = self.nc.dram_tensor(name, list(shape), dt, kind="ExternalOutput")
        self.dram[name] = t
        return t

    def dscr(self, name, shape, dt, kind):
        if kind == "in":
            return self.din(name, shape, dt)
        if kind == "out":
            return self.dout(name, shape, dt)
        t = self.nc.dram_tensor(name, list(shape), dt)
        self.dram[name] = t
        return t

    def carve(self, nbytes):
        o = self.off
        self.off += (nbytes + 63) // 64 * 64
        assert self.off <= self.arena_bytes, (self.off, self.arena_bytes)
        return o

    def view(self, off, shape, dt):
        esz = 4 if dt == F32 else 2
        n = 1
        for s in shape[1:]:
            n *= s
        a = self.arena[:, off // 2: off // 2 + n * esz // 2]
        if dt == F32:
            a = a.bitcast(F32)
        if len(shape) == 2:
            return a
        if len(shape) == 3:
            return a.rearrange("p (a b) -> p a b", a=shape[1])
        if len(shape) == 4:
            return a.rearrange("p (a b c) -> p a b c", a=shape[1], b=shape[2])
        raise ValueError

    def alloc(self, shape, dt):
        esz = 4 if dt == F32 else 2
        n = 1
        for s in shape[1:]:
            n *= s
        return self.view(self.carve(n * esz), shape, dt)

    def setup(self, stack):
        nc = self.nc
        self.arena_bytes = 204 * 1024
        self.arena = stack.enter_context(nc.sbuf_tensor("arena", [128, self.arena_bytes // 2], BF16))
        self.ps = [stack.enter_context(nc.psum_tensor("ps%d" % i, [128, 512], F32)) for i in range(8)]
        P = self.P
        self.ident_f = self.alloc([128, 128], F32)
        self.ident_b = self.alloc([128, 128], BF16)
        self.ones_b = self.alloc([128, 128], BF16)
        self.cst = self.alloc([128, 64], F32)
        c_ident = self.din("c_ident", [128, 128], F32)
        P.add("sp", lambda e: [e.dma_start(out=self.ident_f, in_=c_ident[:, :])], writes=["ident_f"], dma_key="k_c0")
        P.add("dve", lambda e: e.tensor_copy(self.ident_b, self.ident_f), reads=["ident_f"], writes=["ident_b"])
        P.add("dve", lambda e: e.memset(self.ones_b, 1.0), writes=["ones_b"])
        self.phase_base = self.off
        self.xt = [self.alloc([128, NKT, T], F32) for _ in range(2)]
        self.hT = self.alloc([128, NKT, T], BF16)
        self.actT_off = self.carve(NFC * T * 2)
        self.actT = self.view(self.actT_off, [128, NFC, T], BF16)
        self.w_off = [self.carve(22528) for _ in range(2)]
        self.sq = self.alloc([128, 4, T], BF16)
        self.tmp = self.alloc([128, 2, T], F32)
        self.sg = self.alloc([128, 2, T], F32)
        self.rstd = self.alloc([128, T], F32)
        self.modT = self.alloc([128, 144], F32)
        self.modv = self.alloc([128, 9, NKT], F32)
        self.adab = self.alloc([128, 144], F32)
        self.normg = self.alloc([128, 3, NKT], F32)
        self.cs_f = self.alloc([128, NKT], F32)
        self.cs_b = self.alloc([128, NKT], BF16)
        self.wctr = 0
        self.bgw = [self.alloc([128, NKT, 128], BF16) for _ in range(2)]

    bg = None
    bgctr = 0

    def mod_bg_start(self, li, parts, ada_w, ada_b_pm, norm_g_pm):
        P = self.P
        assert self.bg is None
        wsrc = ada_w[li].rearrange("(kt p) n -> p kt n", p=128)
        psm = self.ps[7]

        def gen():
            if getattr(self, "_mod_loaded", None) != li:
                self._mod_loaded = li
                P.add("sp", lambda e: [e.dma_start(out=self.adab, in_=ada_b_pm[li]),
                                       e.dma_start(out=self.normg, in_=norm_g_pm[li])],
                      writes=["adab", "normg"], dma_key="k_c2", ndma=2)
            pending = None
            for p_ in parts:
                for ct in range(48 * p_, 48 * p_ + 48):
                    sl = self.bgctr % 2
                    self.bgctr += 1
                    wv = self.bgw[sl]
                    P.add("pool", lambda e, wv=wv, ct=ct: [e.dma_start(out=wv, in_=wsrc[:, :, ct * 128:(ct + 1) * 128])],
                          writes=[("bgw", sl)], dma_key=("kbg", sl))

                    def mm(e, wv=wv, ct=ct):
                        r = None
                        for kt in range(NKT):
                            r = e.matmul(psm[:, ct:ct + 1], lhsT=wv[:, kt, :], rhs=self.cs_b[:, kt:kt + 1],
                                         start=(kt == 0), stop=(kt == NKT - 1))
                        return r
                    if pending is not None:
                        P.add("pe", pending[0], reads=[("bgw", pending[1]), "cs_b"], writes=[("ps", 7)])
                    pending = (mm, sl)
                    yield
                P.add("pe", pending[0], reads=[("bgw", pending[1]), "cs_b"], writes=[("ps", 7)])
                pending = None
                self._mod_derive(p_)
                yield
        self.bg = gen()

    def bg_step(self, n=1):
        for _ in range(n):
            if self.bg is None:
                return
            try:
                next(self.bg)
            except StopIteration:
                self.bg = None

    def bg_finish(self):
        while self.bg is not None:
            self.bg_step()

    def wview(self, slot, shape):
        return self.view(self.w_off[slot], shape, BF16)

    def next_w(self):
        s = self.wctr % 2
        self.wctr += 1
        return s

    def mod_layer(self, li, c_pm, ada_w, ada_b_pm, norm_g_pm, parts=(0, 1, 2)):
        P = self.P
        if not getattr(self, "_cs_done", False):
            self._cs_done = True
            P.add("sp", lambda e: [e.dma_start(out=self.cs_f, in_=c_pm[:, :])], writes=["cs_f"], dma_key="k_c1")
            P.add("act", lambda e: e.activation(self.cs_b, self.cs_f, AF.Silu), reads=["cs_f"], writes=["cs_b"])
        if getattr(self, "_mod_loaded", None) != li:
            self._mod_loaded = li
            P.add("sp", lambda e: [e.dma_start(out=self.adab, in_=ada_b_pm[li]),
                                   e.dma_start(out=self.normg, in_=norm_g_pm[li])],
                  writes=["adab", "normg"], dma_key="k_c2", ndma=2)
        wsrc = ada_w[li].rearrange("(kt p) n -> p kt n", p=128)
        psm = self.ps[7]
        chunks = [c_ for p_ in parts for c_ in range(12 * p_, 12 * p_ + 12)]
        for ch in chunks:
            s = self.next_w()
            wv = self.wview(s, [128, NKT, 512])
            P.add("pool", lambda e, wv=wv, ch=ch: [e.dma_start(out=wv, in_=wsrc[:, :, ch * 512:(ch + 1) * 512])],
                  writes=[("w", s)], dma_key=("kw", s))

            def mm(e, wv=wv, ch=ch):
                r = None
                for j in range(4):
                    ct = ch * 4 + j
                    for kt in range(NKT):
                        r = e.matmul(psm[:, ct:ct + 1], lhsT=wv[:, kt, j * 128:(j + 1) * 128],
                                     rhs=self.cs_b[:, kt:kt + 1], start=(kt == 0), stop=(kt == NKT - 1))
                return r
            P.add("pe", mm, reads=[("w", s), "cs_b"], writes=[("ps", 7)])
        for p_ in parts:
            self._mod_derive(p_)

    def _mod_derive(self, p_):
        P = self.P
        psm = self.ps[7]
        P.add("dve", lambda e, p_=p_: e.tensor_tensor(self.modT[:, 48 * p_:48 * p_ + 48], psm[:, 48 * p_:48 * p_ + 48],
                                                          self.adab[:, 48 * p_:48 * p_ + 48], ALU.add),
                  reads=[("ps", 7), "adab"], writes=[("modT", p_)])
        sqrtD = math.sqrt(float(D))
        for s3 in (p_,):
            gmul = 1.0 if s3 == 1 else 0.5

            sh = self.modT[:, (3 * s3) * 16:(3 * s3 + 1) * 16]
            sc = self.modT[:, (3 * s3 + 1) * 16:(3 * s3 + 2) * 16]
            gt = self.modT[:, (3 * s3 + 2) * 16:(3 * s3 + 3) * 16]
            A = self.modv[:, 3 * s3 + 0, :]
            B_ = self.modv[:, 3 * s3 + 1, :]
            G = self.modv[:, 3 * s3 + 2, :]
            P.add("dve", lambda e, A=A, sc=sc: e.tensor_scalar(A, sc, 1.0, sqrtD, ALU.add, ALU.mult),
                  reads=[("modT", s3)], writes=[("modv", s3, 0)])
            P.add("dve", lambda e, A=A, s3=s3: e.tensor_tensor(A, A, self.normg[:, s3, :], ALU.mult),
                  reads=[("modv", s3, 0), "normg"], writes=[("modv", s3, 0)])
            P.add("dve", lambda e, B_=B_, sh=sh: e.tensor_copy(B_, sh), reads=[("modT", s3)], writes=[("modv", s3, 1)])
            P.add("dve", lambda e, G=G, gt=gt, gmul=gmul: e.tensor_scalar(G, gt, gmul, None, ALU.mult),
                  reads=[("modT", s3)], writes=[("modv", s3, 2)])

    def load_x_scratch(self, xT_d, t, slot):
        src = xT_d.ap().rearrange("(kt p) t -> p kt t", p=128)[:, :, t * T:(t + 1) * T]
        self.P.add("sp", lambda e: [e.dma_start(out=self.xt[slot], in_=src)],
                   reads=[("xTd", t)], writes=[("xt", slot, k) for k in range(NKT)], dma_key=("kxl", slot))

    def store_x_scratch(self, xT_d, t, slot):
        dst = xT_d.ap().rearrange("(kt p) t -> p kt t", p=128)[:, :, t * T:(t + 1) * T]
        self.P.add("sp", lambda e: [e.dma_start(out=dst, in_=self.xt[slot])],
                   reads=[("xt", slot, k) for k in range(NKT)], writes=[("xTd", t)], dma_key=("kxs", slot))

    def load_x_transpose(self, x_in, t, slot):
        P = self.P
        xin = self.view(self.actT_off, [128, 2, D], F32)
        for u in range(4):
            us = u % 2
            r0 = t * T + u * 128
            P.add("sp", lambda e, us=us, r0=r0: [e.dma_start(out=xin[:, us, :], in_=x_in[r0:r0 + 128, :])],
                  writes=[("actT", f) for f in range(us * 8, us * 8 + 8)], dma_key=("kxin", us))
            for g in range(4):
                bank = 4 + (g % 2)

                def tr(e, us=us, g=g, bank=bank):
                    r = None
                    for j in range(4):
                        kt = g * 4 + j
                        r = e.transpose(self.ps[bank][:, j * 128:(j + 1) * 128], xin[:, us, kt * 128:(kt + 1) * 128],
                                        self.ident_f)
                    return r
                P.add("pe", tr, reads=[("actT", f) for f in range(us * 8, us * 8 + 8)] + ["ident_f"],
                      writes=[("ps", bank)])
                dst = self.xt[slot][:, g * 4:(g + 1) * 4, u * 128:(u + 1) * 128]
                srcp = self.ps[bank][:, :].rearrange("p (a b) -> p a b", a=4)
                eng = "dve" if g % 2 == 0 else "act"
                if eng == "dve":
                    P.add("dve", lambda e, dst=dst, srcp=srcp: e.tensor_copy(dst, srcp), reads=[("ps", bank)],
                          writes=[("xt", slot, g * 4 + j) for j in range(4)])
                else:
                    P.add("act", lambda e, dst=dst, srcp=srcp: e.activation(dst, srcp, AF.Identity),
                          reads=[("ps", bank)], writes=[("xt", slot, g * 4 + j) for j in range(4)])

    def norm_tile(self, slot, s3):
        P = self.P
        xt = self.xt[slot]
        A = self.modv[:, 3 * s3 + 0, :]
        B_ = self.modv[:, 3 * s3 + 1, :]
        pst = self.ps[6]
        for kt in range(NKT):
            q = kt % 4
            P.add("act", lambda e, kt=kt, q=q: e.activation(self.sq[:, q, :], xt[:, kt, :], AF.Square),
                  reads=[("xt", slot, kt)], writes=[("sq", q)])
            P.add("pe", lambda e, kt=kt, q=q: e.matmul(pst[:, :], lhsT=self.ones_b, rhs=self.sq[:, q, :],
                                                       start=(kt == 0), stop=(kt == NKT - 1)),
                  reads=[("sq", q), "ones_b"], writes=[("ps", 6)])
        P.add("act", lambda e: e.activation(self.rstd, pst[:, :], AF.Sqrt, bias=float(D * EPS), scale=1.0),
              reads=[("ps", 6)], writes=["rstd"])
        P.add("dve", lambda e: e.reciprocal(self.rstd, self.rstd), reads=["rstd"], writes=["rstd"])
        for kt in range(NKT):
            q = kt % 2
            P.add("dve", lambda e, kt=kt, q=q: e.tensor_tensor(self.tmp[:, q, :], xt[:, kt, :], self.rstd, ALU.mult),
                  reads=[("xt", slot, kt), "rstd"], writes=[("tmp", q)])
            P.add("act", lambda e, kt=kt, q=q: e.activation(self.hT[:, kt, :], self.tmp[:, q, :], AF.Identity,
                                                            bias=B_[:, kt:kt + 1], scale=A[:, kt:kt + 1]),
                  reads=[("tmp", q), ("modv", s3, 0), ("modv", s3, 1)], writes=[("hT", kt)])

    def ffn_sublayer(self, s3, w_in, w_out, load_fn, store_fn):
        P = self.P
        G = self.modv[:, 3 * s3 + 2, :]
        win = w_in.rearrange("(kt p) n -> p kt n", p=128)
        wout = w_out.rearrange("(fc p) d -> p fc d", p=128)
        for t in range(NT):
            slot = t % 2
            xt = self.xt[slot]
            load_fn(t, slot)
            self.norm_tile(slot, s3)
            for g in range(NFC // 2):
                s = self.next_w()
                wv = self.wview(s, [128, NKT, 512])
                P.add("pool", lambda e, wv=wv, g=g: [
                    e.dma_start(out=wv[:, :, 0:256], in_=win[:, :, g * 256:(g + 1) * 256]),
                    e.dma_start(out=wv[:, :, 256:512], in_=win[:, :, FF + g * 256:FF + (g + 1) * 256])],
                    writes=[("w", s)], dma_key=("kw", s), ndma=2)
                for j in range(2):
                    fc = 2 * g + j
                    pp = fc % 2
                    pa, pb = self.ps[2 * pp], self.ps[2 * pp + 1]

                    def mma(e, wv=wv, j=j, pa=pa, pb=pb):
                        r = None
                        for kt in range(NKT):
                            r = e.matmul(pa[:, :], lhsT=wv[:, kt, j * 128:(j + 1) * 128], rhs=self.hT[:, kt, :],
                                         start=(kt == 0), stop=(kt == NKT - 1))
                        for kt in range(NKT):
                            r = e.matmul(pb[:, :], lhsT=wv[:, kt, 256 + j * 128:256 + (j + 1) * 128],
                                         rhs=self.hT[:, kt, :], start=(kt == 0), stop=(kt == NKT - 1))
                        return r
                    P.add("pe", mma, reads=[("w", s)] + [("hT", k) for k in range(NKT)],
                          writes=[("ps", 2 * pp), ("ps", 2 * pp + 1)])
                    P.add("act", lambda e, pa=pa, pp=pp: e.activation(self.sg[:, pp, :], pa[:, :], AF.Silu),
                          reads=[("ps", 2 * pp)], writes=[("sg", pp)])
                    P.add("dve", lambda e, pb=pb, pp=pp, fc=fc: e.tensor_tensor(self.actT[:, fc, :], self.sg[:, pp, :],
                                                                               pb[:, :], ALU.mult),
                          reads=[("sg", pp), ("ps", 2 * pp + 1)], writes=[("actT", fc)])
                self.bg_step()
            for gd in range(NKT // 2):
                s = self.next_w()
                wv = self.wview(s, [128, NFC, 256])
                P.add("pool", lambda e, wv=wv, gd=gd: [e.dma_start(out=wv, in_=wout[:, :, gd * 256:(gd + 1) * 256])],
                      writes=[("w", s)], dma_key=("kw", s))
                for j in range(2):
                    dt_ = 2 * gd + j
                    py = self.ps[4 + dt_ % 2]

                    def mmb(e, wv=wv, j=j, py=py):
                        r = None
                        for fc in range(NFC):
                            r = e.matmul(py[:, :], lhsT=wv[:, fc, j * 128:(j + 1) * 128], rhs=self.actT[:, fc, :],
                                         start=(fc == 0), stop=(fc == NFC - 1))
                        return r
                    P.add("pe", mmb, reads=[("w", s)] + [("actT", f) for f in range(NFC)], writes=[("ps", 4 + dt_ % 2)])
                    P.add("dve", lambda e, py=py, dt_=dt_, xt=xt: e.scalar_tensor_tensor(
                        xt[:, dt_, :], py[:, :], G[:, dt_:dt_ + 1], xt[:, dt_, :], ALU.mult, ALU.add),
                        reads=[("ps", 4 + dt_ % 2), ("modv", s3, 2), ("xt", slot, dt_)], writes=[("xt", slot, dt_)])
                self.bg_step()
            store_fn(t, slot)
        self.bg_finish()

    after_hT = None

    def da_part1(self, xT_d, hT_d):
        P = self.P
        for t in range(NT):
            slot = t % 2
            self.load_x_scratch(xT_d, t, slot)
            self.norm_tile(slot, 1)
            dst = hT_d[t].ap().rearrange("(kt p) t -> p kt t", p=128)
            P.add("sp", lambda e, dst=dst: [e.dma_start(out=dst, in_=self.hT)],
                  reads=[("hT", k) for k in range(NKT)], writes=[("hTd", t)], dma_key=("khs", 0))
            if self.after_hT is not None:
                self.after_hT(t)


def build_program(stages):
    B = Builder(set(stages))
    nc = B.nc
    P = B.P
    st = B.stages

    def kind(prod, cons):
        if prod in st and all(c in st for c in cons):
            return "int"
        if prod in st:
            return "out"
        return "in"

    stack = contextlib.ExitStack()
    with stack:
        B.setup(stack)
        need_mod0 = (0 in st) or (2 in st)
        need_mod1 = (2 in st) or (3 in st)
        class _Lazy:
            def __init__(self, name, shape):
                self.name, self.shape, self.c = name, shape, {}

            def __getitem__(self, idx):
                if not isinstance(idx, tuple):
                    idx = (idx,)
                if idx not in self.c:
                    self.c[idx] = B.din(self.name + "_" + "".join(str(i) for i in idx), self.shape).ap()
                return self.c[idx]
        if need_mod0 or need_mod1:
            c_pm = B.din("c_pm", [128, NKT])
            ada_w = _Lazy("ada_w", [D, 9 * D])
            ada_b_pm = B.din("ada_b_pm", [2, 128, 144])
            norm_g_pm = B.din("norm_g_pm", [2, 128, 3, NKT])
        ffn_w_in = _Lazy("ffn_w_in", [D, 2 * FF])
        ffn_w_out = _Lazy("ffn_w_out", [FF, D])
        PAIRS = [[0, 1], [2, 3], [4, 5], [6, 7]]

        def all_gather(src, dst, src_res, name):
            P.add("pool", lambda e: [e.collective_compute("AllGather", ALU.bypass, replica_groups=PAIRS,
                                                          ins=[src.ap().opt()], outs=[dst.ap().opt()])],
                  reads=list(src_res) + ["cc_chain"], writes=[name, "cc_chain"], dma_key="k_" + name, unit=CC_UNIT)
        finals = []
        if 0 in st:
            x_in = B.din("x_in", [TOK, D])
            xT0 = B.dscr("xT0", [D, TOK], F32, kind(0, [2]))
            hT_own = [B.dscr("hT_own_%d" % t, [D, T], BF16, kind(0, [1])) for t in range(NT)]
            if 1 in st:
                hT_all = [B.dscr("hT_all_%d" % t, [2 * D, T], BF16, "int") for t in range(NT)]
                B.after_hT = lambda t: all_gather(hT_own[t], hT_all[t], [("hTd", t)], "cc_h%d" % t)
            B.mod_layer(0, c_pm, ada_w, ada_b_pm, norm_g_pm, parts=(0,))

            def st0(t, slot):
                B.store_x_scratch(xT0, t, slot)
            B.mod_bg_start(0, (1, 2), ada_w, ada_b_pm, norm_g_pm)
            B.ffn_sublayer(0, ffn_w_in[0, 0], ffn_w_out[0, 0],
                           lambda t, slot: B.load_x_transpose(x_in, t, slot), st0)
            B.da_part1(xT0, hT_own)
            if kind(0, [2]) == "out":
                finals += [("kxs", 0), ("kxs", 1), ("khs", 0)]
        if 1 in st:
            if 0 in st:
                P.barrier()
            else:
                hT_all = [B.dscr("hT_all_%d" % t, [2 * D, T], BF16, "in") for t in range(NT)]
            da_wqkv_my = B.din("da_wqkv_my", [D, 3072])
            da_bias = B.din("da_bias", [4, 3, 128, QT])
            da_offs = B.din("da_offs", [128, 128])
            da_lam_b = B.din("da_lam_b", [128, 512])
            da_gvec_b = B.din("da_gvec_b", [128, 256])
            O_heads = [B.dscr("O_heads_%d" % k, [1024, 1024], BF16, kind(1, [2])) for k in range(4)]
            da_part2(B, hT_all, da_wqkv_my, da_bias, da_offs, da_lam_b, da_gvec_b, O_heads)
            if kind(1, [2]) == "out":
                finals += [("k_oo", 0), ("k_oo", 1)]
        if 2 in st:
            sel_b = B.din("sel_b", [128, 1])
            O_all = [B.dscr("O_all_%d" % k, [2048, 1024], BF16, "int" if 1 in st else "in") for k in range(4)]
            if 1 in st:
                P.barrier()
                for k in range(4):
                    all_gather(O_heads[k], O_all[k], [], "cc_o%d" % k)
            if 0 not in st:
                xT0 = B.din("xT0", [D, TOK], F32)
                B.mod_layer(0, c_pm, ada_w, ada_b_pm, norm_g_pm)
            else:
                P.barrier()
            da_w_o = B.din("da_w_o", [D, D])
            ret_gng_b = B.din("ret_gng_b", [128, 512])
            ret_w = B.din("ret_w_qkvg", [D, 12288])
            kdec_glob = B.din("ret_kdec_glob", [128, 8, 16])
            xTa = B.dscr("xTa", [D, TOK], F32, "int")
            xTb = B.dscr("xTb", [D, TOK], F32, kind(2, [3]))
            k23 = kind(2, [3])
            QTd = B.dscr("ret_QT", [D, TOK], BF16, k23)
            KTd = B.dscr("ret_KT", [D, TOK], BF16, k23)
            Ktd = B.dscr("ret_Kt", [TOK, D], BF16, k23)
            Vd = B.dscr("ret_V", [TOK, 2 * D], BF16, k23)
            Gd = B.dscr("ret_G", [TOK, 2 * D], BF16, k23)
            S_loc = [B.dscr("S_loc_%d" % k, [1024, 512], F32, k23) for k in range(2)]
            da3_setup(B, sel_b)
            def st2a(t, slot):
                B.store_x_scratch(xTa, t, slot)

            def st2b(t, slot):
                B.store_x_scratch(xTb, t, slot)
            B.mod_bg_start(1, (0,), ada_w, ada_b_pm, norm_g_pm)
            B.ffn_sublayer(2, ffn_w_in[0, 1], ffn_w_out[0, 1],
                           lambda t, slot: da3_load(B, xT0, O_all, da_w_o[:, :], t, slot), st2a)
            B.mod_bg_start(1, (1, 2), ada_w, ada_b_pm, norm_g_pm)
            B.ffn_sublayer(0, ffn_w_in[1, 0], ffn_w_out[1, 0],
                           lambda t, slot: B.load_x_scratch(xTa, t, slot), st2b)
            ret_part1(B, xTb, ret_w[:, :], QTd, KTd, Ktd, Vd, Gd, ret_gng_b)
            P.barrier()
            ret_part2(B, Ktd, Vd, kdec_glob, S_loc)
            if k23 == "out":
                finals += [("kxs", 0), ("kxs", 1), ("k_rs", 0), ("k_rs", 1), ("k_r2s", 0), ("k_r2s", 1)]
        if 3 in st:
            if 2 not in st:
                sel_b = B.din("sel_b", [128, 1])
                xTb = B.din("xTb", [D, TOK], F32)
                QTd = B.din("ret_QT", [D, TOK], BF16)
                KTd = B.din("ret_KT", [D, TOK], BF16)
                Ktd = B.din("ret_Kt", [TOK, D], BF16)
                Vd = B.din("ret_V", [TOK, 2 * D], BF16)
                Gd = B.din("ret_G", [TOK, 2 * D], BF16)
                B.mod_layer(1, c_pm, ada_w, ada_b_pm, norm_g_pm)
            S_all = [B.dscr("S_all_%d" % k, [2048, 512], F32, "int" if 2 in st else "in") for k in range(2)]
            if 2 in st:
                P.barrier()
                for k in range(2):
                    all_gather(S_loc[k], S_all[k], [], "cc_s%d" % k)
            ret_dmask = B.din("ret_dmask", [128, 8, 128])
            ret_qdec = B.din("ret_qdec", [128, 8])
            ret_kdec = B.din("ret_kdec", [128, 8])
            if 2 not in st:
                ret_gng_b = B.din("ret_gng_b", [128, 512])
            ret_w_o = B.din("ret_w_o", [2 * D, D])
            fg_pm = B.din("final_g_pm", [128, NKT])
            out_d = B.dout("out", [TOK, D], F32)
            OGd = B.dscr("ret_OG", [TOK, 2 * D], BF16, "out" if DEBUG.get("r3only") else ("in" if DEBUG.get("r4only") else "int"))
            fg_sb = B.alloc([128, NKT], F32)
            P.add("sp", lambda e: [e.dma_start(out=fg_sb, in_=fg_pm[:, :])], writes=["fg"], dma_key="k_fg")
            P.add("dve", lambda e: e.tensor_scalar(fg_sb, fg_sb, math.sqrt(float(D)), None, ALU.mult), reads=["fg"], writes=["fg"])
            P.barrier()
            if not DEBUG.get("r4only"):
                ret_part3(B, QTd, KTd, Ktd, Vd, Gd, S_all, sel_b, ret_dmask, ret_qdec, ret_kdec, ret_gng_b, OGd)
                P.barrier()
            if DEBUG.get("r3only"):
                finals += [("k_r3s", 0), ("k_r3s", 1)]
            else:
                B.ffn_sublayer(2, ffn_w_in[1, 1], ffn_w_out[1, 1],
                               lambda t, slot: ret4_load(B, xTb, OGd, ret_w_o[:, :], t, slot),
                               lambda t, slot: final_store(B, out_d, fg_sb, t, slot))
                finals += ["k_out"]
        P.final_keys = finals
        with nc.allow_low_precision("bf16 matmul operands, fp32 accumulation"), \
                nc.allow_non_contiguous_dma("weight tile streaming"):
            with nc.Block() as block:
                P.emit(nc, block, stack)
    return nc


QT = 256
NQT = SEQ // QT
NKT_SEQ = SEQ // 128


def da_part2(B, hT_all, da_wqkv_my, da_bias, da_offs, da_lam_b, da_gvec_b, O_heads):
    P = B.P
    base = B.phase_base
    off = base

    def al(shape, dt):
        nonlocal off
        esz = 4 if dt == F32 else 2
        n = 1
        for s in shape[1:]:
            n *= s
        v = B.view(off, shape, dt)
        off += (n * esz + 63) // 64 * 64
        assert off <= B.arena_bytes, off
        return v
    hs = [al([128, NKT, T], BF16) for _ in range(2)]
    wqk = al([128, NKT, 512], BF16)
    wv = al([128, NKT, 256], BF16)
    qT = al([128, 2, SEQ], BF16)
    kT = al([128, 2, SEQ], BF16)
    V = al([128, NKT_SEQ, 264], BF16)
    bias = al([128, 3, 2 * QT], F32)
    s_sb = al([128, 4, 2 * QT], F32)
    pT = al([128, 4, 2 * QT], BF16)
    offs = al([128, 128], F32)
    lamb = al([128, 512], F32)
    lprod = al([128, 256], F32)
    lsum = al([128, 2], F32)
    neglam = al([128, 1], F32)
    gvec = al([128, 256], F32)
    rden = al([128, 4], F32)
    o_sb = al([128, 2, 256], F32)
    osq = al([128, 2, 256], F32)
    ssq = al([128, 2], F32)
    o_out = al([128, 2, 256], BF16)
    ps = B.ps
    lambda_init = 0.8 - 0.6 * math.exp(-0.3 * 0)

    P.add("sp", lambda e: [e.dma_start(out=offs, in_=da_offs[:, :]),
                           e.dma_start(out=lamb, in_=da_lam_b[:, :]),
                           e.dma_start(out=gvec, in_=da_gvec_b[:, :])],
          writes=["offs", "lamb", "gvec"], dma_key="k_dac", ndma=3)
    P.add("dve", lambda e: e.memset(V[:, :, 256:257], 1.0), writes=["Vones"])
    P.add("dve", lambda e: e.tensor_scalar(gvec, gvec, 1.0 - lambda_init, None, ALU.mult), reads=["gvec"], writes=["gvec"])
    P.add("dve", lambda e: e.tensor_tensor(lprod[:, 0:128], lamb[:, 0:128], lamb[:, 128:256], ALU.mult),
          reads=["lamb"], writes=["lprod0"])
    P.add("dve", lambda e: e.tensor_tensor(lprod[:, 128:256], lamb[:, 256:384], lamb[:, 384:512], ALU.mult),
          reads=["lamb"], writes=["lprod1"])
    P.add("dve", lambda e: e.reduce_sum(lsum[:, 0:1], lprod[:, 0:128], axis=AX.X), reads=["lprod0"], writes=["lsum0"])
    P.add("dve", lambda e: e.reduce_sum(lsum[:, 1:2], lprod[:, 128:256], axis=AX.X), reads=["lprod1"], writes=["lsum1"])
    P.add("act", lambda e: e.activation(lsum, lsum, AF.Exp), reads=["lsum0", "lsum1"], writes=["lsum0", "lsum1"])
    P.add("dve", lambda e: e.tensor_tensor(neglam, lsum[:, 1:2], lsum[:, 0:1], ALU.subtract),
          reads=["lsum0", "lsum1"], writes=["neglam"])
    P.add("dve", lambda e: e.tensor_scalar(neglam, neglam, -lambda_init, None, ALU.add), reads=["neglam"], writes=["neglam"])

    wsrc = da_wqkv_my.ap().rearrange("(kt p) n -> p kt n", p=128)
    qscale = 128 ** -0.5
    for hh in range(4):
        P.add("pool", lambda e, hh=hh: [
            e.dma_start(out=wqk[:, :, 0:256], in_=wsrc[:, :, hh * 256:(hh + 1) * 256]),
            e.dma_start(out=wqk[:, :, 256:512], in_=wsrc[:, :, 1024 + hh * 256:1024 + (hh + 1) * 256]),
            e.dma_start(out=wv, in_=wsrc[:, :, 2048 + hh * 256:2048 + (hh + 1) * 256])],
            writes=["wqk", "wv"], dma_key="k_daw", ndma=3)
        P.add("sp", lambda e, hh=hh: [e.dma_start(out=bias[:, :, 0:QT], in_=da_bias[hh].rearrange("a p q -> p a q")),
                                      e.dma_start(out=bias[:, :, QT:2 * QT], in_=da_bias[hh].rearrange("a p q -> p a q"))],
              writes=["dabias"], dma_key="k_dab", ndma=2)
        pjc = 0
        for tt in range(SEQ // T):
            hsl = tt % 2
            rho, lt = tt // NT, tt % NT
            src = hT_all[lt].ap()[rho * D:(rho + 1) * D, :].rearrange("(kt p) t -> p kt t", p=128)
            P.add("sp", lambda e, src=src, hsl=hsl: [e.dma_start(out=hs[hsl], in_=src)],
                  writes=[("hs", hsl)], dma_key=("k_hs", hsl))
            for m in range(4):
                bank = [0, 1, 6, 7][pjc % 4]
                pjc += 1

                def mmq(e, m=m, hsl=hsl, bank=bank):
                    r = None
                    for kt in range(NKT):
                        r = e.matmul(ps[bank][:, :], lhsT=wqk[:, kt, m * 128:(m + 1) * 128], rhs=hs[hsl][:, kt, :],
                                     start=(kt == 0), stop=(kt == NKT - 1))
                    return r
                P.add("pe", mmq, reads=["wqk", ("hs", hsl)], writes=[("ps", bank)])
                if m < 2:
                    P.add("act", lambda e, m=m, tt=tt, bank=bank: e.activation(
                        qT[:, m, tt * T:(tt + 1) * T], ps[bank][:, :], AF.Identity, scale=qscale),
                        reads=[("ps", bank)], writes=[("qT", tt)])
                else:
                    P.add("dve", lambda e, m=m, tt=tt, bank=bank: e.tensor_copy(
                        kT[:, m - 2, tt * T:(tt + 1) * T], ps[bank][:, :]),
                        reads=[("ps", bank)], writes=[("kT", tt)])
            for u2 in range(2):
                bank = [0, 1, 6, 7][pjc % 4]
                pjc += 1

                def mmv(e, u2=u2, hsl=hsl, bank=bank):
                    r = None
                    for uu in range(2):
                        u = u2 * 2 + uu
                        for kt in range(NKT):
                            r = e.matmul(ps[bank][:, uu * 256:(uu + 1) * 256], lhsT=hs[hsl][:, kt, u * 128:(u + 1) * 128],
                                         rhs=wv[:, kt, :], start=(kt == 0), stop=(kt == NKT - 1))
                    return r
                P.add("pe", mmv, reads=["wv", ("hs", hsl)], writes=[("ps", bank)])
                i0 = tt * 4 + u2 * 2
                eng = "act" if u2 == 0 else "dve"
                dstv = V[:, i0:i0 + 2, 0:256]
                srcv = ps[bank][:, :].rearrange("p (a b) -> p a b", a=2)
                if eng == "act":
                    P.add("act", lambda e, dstv=dstv, srcv=srcv: e.activation(dstv, srcv, AF.Identity),
                          reads=[("ps", bank)], writes=[("V", i0), ("V", i0 + 1)])
                else:
                    P.add("dve", lambda e, dstv=dstv, srcv=srcv: e.tensor_copy(dstv, srcv),
                          reads=[("ps", bank)], writes=[("V", i0), ("V", i0 + 1)])
        SB = [0, 1, 6, 7]
        LA = 3
        units = [(j, i) for j in range(NQT) for i in range(2 * (j + 1))]

        def emit_scores(idx, j, i):
            sl = idx % 4
            bank = SB[sl]

            def mms(e, i=i, j=j, bank=bank):
                r = None
                for m in range(2):
                    r = e.matmul(ps[bank][:, m * QT:(m + 1) * QT], lhsT=kT[:, m, i * 128:(i + 1) * 128],
                                 rhs=qT[:, m, j * QT:(j + 1) * QT], start=True, stop=True)
                return r
            P.add("pe", mms, reads=[("kT", i // 4), ("qT", j // 2)], writes=[("ps", bank)])
            bt = 0 if i < 2 * j else 1 + (i - 2 * j)
            P.add("dve", lambda e, sl=sl, bt=bt, bank=bank: e.tensor_tensor(s_sb[:, sl, :], ps[bank][:, :], bias[:, bt, :], ALU.add),
                  reads=[("ps", bank), "dabias"], writes=[("s_sb", sl)])
            if i < 2 * j:
                bcol = offs[:, hh * 32 + (2 * j - i):hh * 32 + (2 * j - i) + 1]
                P.add("act", lambda e, sl=sl, bcol=bcol: e.activation(pT[:, sl, :], s_sb[:, sl, :], AF.Exp, bias=bcol),
                      reads=[("s_sb", sl), "offs"], writes=[("pT", sl)])
            else:
                P.add("act", lambda e, sl=sl: e.activation(pT[:, sl, :], s_sb[:, sl, :], AF.Exp),
                      reads=[("s_sb", sl)], writes=[("pT", sl)])

        def emit_pv(idx, j, i):
            sl = idx % 4
            nk = 2 * (j + 1)

            def pv(e, sl=sl, i=i, nk=nk):
                r = None
                for m in range(2):
                    for u in range(2):
                        r = e.matmul(ps[2 + m * 2 + u][:, 0:257], lhsT=pT[:, sl, m * QT + u * 128:m * QT + (u + 1) * 128],
                                     rhs=V[:, i, 0:257], start=(i == 0), stop=(i == nk - 1))
                return r
            P.add("pe", pv, reads=[("pT", sl), ("V", i), "Vones"], writes=[("acc", m, u) for m in range(2) for u in range(2)])
            if i == nk - 1:
                epilogue(j)

        def epilogue(j):
            for u in range(2):
                a0 = ps[2 + u]
                a1 = ps[4 + u]
                P.add("dve", lambda e, a0=a0, u=u: e.reciprocal(rden[:, 2 * u:2 * u + 1], a0[:, 256:257]),
                      reads=[("acc", 0, u)], writes=[("rden", u, 0)])
                P.add("dve", lambda e, a1=a1, u=u: e.reciprocal(rden[:, 2 * u + 1:2 * u + 2], a1[:, 256:257]),
                      reads=[("acc", 1, u)], writes=[("rden", u, 1)])
                P.add("dve", lambda e, u=u: e.tensor_tensor(rden[:, 2 * u + 1:2 * u + 2], rden[:, 2 * u + 1:2 * u + 2], neglam, ALU.mult),
                      reads=[("rden", u, 1), "neglam"], writes=[("rden", u, 1)])
                P.add("dve", lambda e, a0=a0, u=u: e.tensor_scalar(o_sb[:, u, :], a0[:, 0:256], rden[:, 2 * u:2 * u + 1], None, ALU.mult),
                      reads=[("acc", 0, u), ("rden", u, 0)], writes=[("o_sb", u)])
                P.add("dve", lambda e, a1=a1, u=u: e.scalar_tensor_tensor(o_sb[:, u, :], a1[:, 0:256], rden[:, 2 * u + 1:2 * u + 2],
                                                                            o_sb[:, u, :], ALU.mult, ALU.add),
                      reads=[("acc", 1, u), ("rden", u, 1), ("o_sb", u)], writes=[("o_sb", u)])
                P.add("dve", lambda e, u=u: e.tensor_tensor(osq[:, u, :], o_sb[:, u, :], o_sb[:, u, :], ALU.mult),
                      reads=[("o_sb", u)], writes=[("osq", u)])
                P.add("dve", lambda e, u=u: e.reduce_sum(ssq[:, u:u + 1], osq[:, u, :], axis=AX.X),
                      reads=[("osq", u)], writes=[("ssq", u)])
                P.add("act", lambda e, u=u: e.activation(ssq[:, u:u + 1], ssq[:, u:u + 1], AF.Sqrt, bias=float(EPS), scale=1.0 / 256),
                      reads=[("ssq", u)], writes=[("ssq", u)])
                P.add("dve", lambda e, u=u: e.reciprocal(ssq[:, u:u + 1], ssq[:, u:u + 1]), reads=[("ssq", u)], writes=[("ssq", u)])
                P.add("dve", lambda e, u=u: e.scalar_tensor_tensor(o_out[:, u, :], o_sb[:, u, :], ssq[:, u:u + 1], gvec,
                                                                   ALU.mult, ALU.mult),
                      reads=[("o_sb", u), ("ssq", u), "gvec"], writes=[("o_out", u)])
                r0 = j * QT + u * 128
                P.add("sp", lambda e, u=u, r0=r0, hh=hh: [e.dma_start(
                    out=O_heads[r0 // 1024][r0 % 1024:r0 % 1024 + 128, hh * 256:(hh + 1) * 256], in_=o_out[:, u, :])],
                      reads=[("o_out", u)], writes=[("Ohd", hh, j, u)], dma_key=("k_oo", u))


        for idx, (j, i) in enumerate(units):
            emit_scores(idx, j, i)
            if idx >= LA:
                emit_pv(idx - LA, *units[idx - LA])
        for idx in range(max(0, len(units) - LA), len(units)):
            emit_pv(idx, *units[idx])

def da3_setup(B, sel_b):
    P = B.P
    B.sel = B.alloc([128, 1], F32)
    B.I1f = B.alloc([128, 128], F32)
    B.I0b = B.alloc([128, 128], BF16)
    B.I1b = B.alloc([128, 128], BF16)
    P.add("sp", lambda e: [e.dma_start(out=B.sel, in_=sel_b[:, :])], writes=["sel"], dma_key="k_sel")
    P.add("dve", lambda e: e.tensor_scalar(B.I1f, B.ident_f, B.sel[:, 0:1], None, ALU.mult), reads=["sel", "ident_f"], writes=["I1f"])
    P.add("dve", lambda e: e.tensor_copy(B.I1b, B.I1f), reads=["I1f"], writes=["I1b"])
    P.add("dve", lambda e: e.tensor_tensor(B.I0b, B.ident_f, B.I1f, ALU.subtract), reads=["I1f", "ident_f"], writes=["I0b"])


def da3_load(B, xT0, O_all, da_w_o, t, slot):
    P = B.P
    ps = B.ps
    B.load_x_scratch(xT0, t, slot)
    ob = B.view(B.actT_off, [128, 4, 4, 1024], BF16)
    for q in range(4):
        rho, half = q // 2, q % 2
        base = rho * 1024 + (t % 2) * T
        src = O_all[2 * half + t // 2][base:base + T, :].rearrange("(u p) e -> p u e", p=128)
        P.add("sp", lambda e, q=q, src=src: [e.dma_start(out=ob[:, q], in_=src)],
              writes=[("actT", f) for f in range(8 * q, 8 * q + 8)], dma_key=("k_ob", q))
    for et in range(NKT):
        rho, el = et // 8, et % 8
        bank = 4 + et % 2

        def tr(e, rho=rho, el=el, bank=bank):
            r = None
            for u in range(4):
                o_ = ps[bank][:, u * 128:(u + 1) * 128]
                e.matmul(o_, lhsT=ob[:, 2 * rho, u, el * 128:(el + 1) * 128], rhs=B.I0b, start=True, stop=False)
                r = e.matmul(o_, lhsT=ob[:, 2 * rho + 1, u, el * 128:(el + 1) * 128], rhs=B.I1b, start=False, stop=True)
            return r
        P.add("pe", tr, reads=[("actT", f) for f in range(16 * rho, 16 * rho + 16)] + ["I0b", "I1b"], writes=[("ps", bank)])
        if et % 2 == 0:
            P.add("act", lambda e, et=et, bank=bank: e.activation(B.hT[:, et, :], ps[bank][:, :], AF.Identity),
                  reads=[("ps", bank)], writes=[("hT", et)])
        else:
            P.add("dve", lambda e, et=et, bank=bank: e.tensor_copy(B.hT[:, et, :], ps[bank][:, :]),
                  reads=[("ps", bank)], writes=[("hT", et)])
    mixer_out_proj(B, da_w_o, NKT, B.hT, [("hT", k) for k in range(NKT)], slot)


def mixer_out_proj(B, w_o, n_et, OT, OT_res, slot):
    P = B.P
    ps = B.ps
    G2 = B.modv[:, 3 * 1 + 2, :]
    xt = B.xt[slot]
    wsrc = w_o.rearrange("(et p) d -> p et d", p=128)
    for gd in range(NKT // 2):
        s = B.next_w()
        wv = B.wview(s, [128, n_et, 256])
        P.add("pool", lambda e, wv=wv, gd=gd: [e.dma_start(out=wv, in_=wsrc[:, :, gd * 256:(gd + 1) * 256])],
              writes=[("w", s)], dma_key=("kw", s))
        for j in range(2):
            dt_ = 2 * gd + j
            py = ps[4 + dt_ % 2]

            def mmo(e, wv=wv, j=j, py=py):
                r = None
                for et in range(n_et):
                    r = e.matmul(py[:, :], lhsT=wv[:, et, j * 128:(j + 1) * 128], rhs=OT[:, et, :],
                                 start=(et == 0), stop=(et == n_et - 1))
                return r
            P.add("pe", mmo, reads=[("w", s)] + OT_res, writes=[("ps", 4 + dt_ % 2)])
            P.add("dve", lambda e, py=py, dt_=dt_: e.scalar_tensor_tensor(
                xt[:, dt_, :], py[:, :], G2[:, dt_:dt_ + 1], xt[:, dt_, :], ALU.mult, ALU.add),
                reads=[("ps", 4 + dt_ % 2), ("modv", 1, 2), ("xt", slot, dt_)], writes=[("xt", slot, dt_)])


def ret_part1(B, xT_d, ret_w, QTd, KTd, Ktd, Vd, Gd, gng_b):
    P = B.P
    ps = B.ps
    gng_sb = B.alloc([128, 512], F32)
    P.add("sp", lambda e: [e.dma_start(out=gng_sb, in_=gng_b[:, :])], writes=["gng1"], dma_key="k_gng1")
    wsrc = ret_w.rearrange("(kt p) n -> p kt n", p=128)
    stg = B.view(B.actT_off, [128, 2, 4, 512], BF16)
    kscale = 256 ** -0.5
    cnt = 0
    for t in range(NT):
        slot = t % 2
        B.load_x_scratch(xT_d, t, slot)
        B.norm_tile(slot, 1)
        hres = [("hT", k) for k in range(NKT)]
        for g in range(24):
            s = B.next_w()
            wv = B.wview(s, [128, NKT, 512])
            P.add("pool", lambda e, wv=wv, g=g: [e.dma_start(out=wv, in_=wsrc[:, :, g * 512:(g + 1) * 512])],
                  writes=[("w", s)], dma_key=("kw", s))
            if g < 8:
                ss = cnt % 2
                cnt += 1
                for j in range(4):
                    bank = j % 2

                    def mmf(e, wv=wv, j=j, bank=bank):
                        r = None
                        for kt in range(NKT):
                            r = e.matmul(ps[bank][:, :], lhsT=wv[:, kt, j * 128:(j + 1) * 128], rhs=B.hT[:, kt, :],
                                         start=(kt == 0), stop=(kt == NKT - 1))
                        return r
                    P.add("pe", mmf, reads=[("w", s)] + hres, writes=[("ps", bank)])
                    sc = 1.0 if g < 4 else kscale
                    P.add("act", lambda e, ss=ss, j=j, bank=bank, sc=sc: e.activation(
                        stg[:, ss, j, :], ps[bank][:, :], AF.Identity, scale=sc),
                        reads=[("ps", bank)], writes=[("actT", ss * 4 + j)])
                dstT = (QTd if g < 4 else KTd)
                r0 = (g % 4) * 512
                dst = dstT[r0:r0 + 512, t * T:(t + 1) * T].rearrange("(j p) t -> p j t", p=128)
                P.add("sp", lambda e, dst=dst, ss=ss: [e.dma_start(out=dst, in_=stg[:, ss])],
                      reads=[("actT", ss * 4 + j) for j in range(4)], writes=[("retTd", g, t)], dma_key=("k_rs", ss))
            if g >= 4:
                ss = cnt % 2
                cnt += 1
                for u in range(4):
                    bank = 2 + u % 2

                    def mmt(e, wv=wv, u=u, bank=bank):
                        r = None
                        for kt in range(NKT):
                            r = e.matmul(ps[bank][:, :], lhsT=B.hT[:, kt, u * 128:(u + 1) * 128], rhs=wv[:, kt, :],
                                         start=(kt == 0), stop=(kt == NKT - 1))
                        return r
                    P.add("pe", mmt, reads=[("w", s)] + hres, writes=[("ps", bank)])
                    if g < 8:
                        P.add("act", lambda e, ss=ss, u=u, bank=bank: e.activation(
                            stg[:, ss, u, :], ps[bank][:, :], AF.Identity, scale=kscale),
                            reads=[("ps", bank)], writes=[("actT", ss * 4 + u)])
                    elif g < 16:
                        P.add("dve", lambda e, ss=ss, u=u, bank=bank: e.tensor_copy(stg[:, ss, u, :], ps[bank][:, :]),
                              reads=[("ps", bank)], writes=[("actT", ss * 4 + u)])
                    else:
                        P.add("act", lambda e, ss=ss, u=u, bank=bank: e.activation(
                            stg[:, ss, u, :], ps[bank][:, :], AF.Silu),
                            reads=[("ps", bank)], writes=[("actT", ss * 4 + u)])
                        P.add("dve", lambda e, ss=ss, u=u: e.tensor_tensor(stg[:, ss, u, :], stg[:, ss, u, :], gng_sb, ALU.mult),
                              reads=[("actT", ss * 4 + u), "gng1"], writes=[("actT", ss * 4 + u)])
                if g < 8:
                    dd, c0 = Ktd, (g - 4) * 512
                elif g < 16:
                    dd, c0 = Vd, (g - 8) * 512
                else:
                    dd, c0 = Gd, (g - 16) * 512
                dst = dd[t * T:(t + 1) * T, c0:c0 + 512].rearrange("(u p) c -> p u c", p=128)
                P.add("sp", lambda e, dst=dst, ss=ss: [e.dma_start(out=dst, in_=stg[:, ss])],
                      reads=[("actT", ss * 4 + u) for u in range(4)], writes=[("retd", g, t)], dma_key=("k_rs", ss))


def ret_part2(B, Ktd, Vd, kdec_glob, S_loc):
    P = B.P
    ps = B.ps
    off = B.phase_base

    def al(shape, dt):
        nonlocal off
        esz = 4 if dt == F32 else 2
        n = 1
        for s in shape[1:]:
            n *= s
        v = B.view(off, shape, dt)
        off += (n * esz + 63) // 64 * 64
        assert off <= B.arena_bytes, off
        return v
    kb = [al([128, 16, 256], BF16) for _ in range(2)]
    kd = [al([128, 16, 256], BF16) for _ in range(2)]
    vb = [al([128, 16, 512], BF16) for _ in range(2)]
    kdec = al([128, 8, 16], F32)
    sst = [al([128, 2, 512], F32) for _ in range(2)]
    P.add("sp", lambda e: [e.dma_start(out=kdec, in_=kdec_glob[:, :, :])], writes=["kdecg"], dma_key="k_kdg")
    for h in range(RET_H):
        b_ = h % 2
        ksrc = Ktd[:, h * 256:(h + 1) * 256].rearrange("(st p) c -> p st c", p=128)
        vsrc = Vd[:, h * 512:(h + 1) * 512].rearrange("(st p) c -> p st c", p=128)
        P.add("sp", lambda e, b_=b_, ksrc=ksrc, vsrc=vsrc: [e.dma_start(out=kb[b_], in_=ksrc), e.dma_start(out=vb[b_], in_=vsrc)],
              reads=["retd_all"], writes=[("kb", b_), ("vb", b_)], dma_key=("k_r2", b_), ndma=2)
        for st in range(16):
            P.add("dve", lambda e, b_=b_, st=st, h=h: e.tensor_scalar(kd[b_][:, st, :], kb[b_][:, st, :],
                                                                        kdec[:, h, st:st + 1], None, ALU.mult),
                  reads=[("kb", b_), "kdecg"], writes=[("kd", b_, st)])
        for dkt in range(2):
            bank = dkt

            def mms(e, b_=b_, dkt=dkt, bank=bank):
                r = None
                for st in range(16):
                    r = e.matmul(ps[bank][:, :], lhsT=kd[b_][:, st, dkt * 128:(dkt + 1) * 128], rhs=vb[b_][:, st, :],
                                 start=(st == 0), stop=(st == 15))
                return r
            P.add("pe", mms, reads=[("kd", b_, st) for st in range(16)] + [("vb", b_)], writes=[("ps", bank)])
            if dkt == 0:
                P.add("act", lambda e, b_=b_, dkt=dkt, bank=bank: e.activation(sst[b_][:, dkt, :], ps[bank][:, :], AF.Identity),
                      reads=[("ps", bank)], writes=[("sst", b_, dkt)])
            else:
                P.add("dve", lambda e, b_=b_, dkt=dkt, bank=bank: e.tensor_copy(sst[b_][:, dkt, :], ps[bank][:, :]),
                      reads=[("ps", bank)], writes=[("sst", b_, dkt)])
        dst = S_loc[h // 4][(h % 4) * 256:(h % 4 + 1) * 256, :].rearrange("(a p) c -> p a c", p=128)
        P.add("sp", lambda e, b_=b_, dst=dst: [e.dma_start(out=dst, in_=sst[b_])],
              reads=[("sst", b_, 0), ("sst", b_, 1)], writes=[("Slocd", h)], dma_key=("k_r2s", b_))


def ret_part3(B, QTd, KTd, Ktd, Vd, Gd, S_all, sel_b, ret_dmask, ret_qdec, ret_kdec, ret_gng_b, OGd):
    P = B.P
    ps = B.ps
    off = B.phase_base

    def al(shape, dt):
        nonlocal off
        esz = 4 if dt == F32 else 2
        n = 1
        for s in shape[1:]:
            n *= s
        v = B.view(off, shape, dt)
        off += (n * esz + 63) // 64 * 64
        assert off <= B.arena_bytes, off
        return v
    qn = [al([128, 16, 128], BF16) for _ in range(2)]
    kn = [al([128, 16, 128], BF16) for _ in range(2)]
    ktn = [al([128, 2048], BF16) for _ in range(2)]
    vn = [al([128, 4096], BF16) for _ in range(2)]
    gn = [al([128, 4096], BF16) for _ in range(2)]
    ogn = [al([128, 4096], BF16) for _ in range(2)]
    S_f = al([128, 16, 512], F32)
    S_b = al([128, 16, 512], BF16)
    dmask = al([128, 8, 128], F32)
    qdec = al([128, 8], F32)
    kdec = al([128, 8], F32)
    gng = al([128, 512], F32)
    sel = al([128, 1], F32)
    at_sb = al([128, 2, 128], BF16)
    kd_sb = al([128, 2, 256], BF16)
    oi_sb = al([128, 2, 512], F32)
    o_sb = al([128, 2, 512], F32)
    osq = al([128, 2, 512], F32)
    y_sb = al([128, 2, 512], F32)
    st = al([128, 2, 8], F32)
    P.add("sp", lambda e: [e.dma_start(out=dmask, in_=ret_dmask[:, :, :]), e.dma_start(out=qdec, in_=ret_qdec[:, :]),
                           e.dma_start(out=kdec, in_=ret_kdec[:, :]), e.dma_start(out=gng, in_=ret_gng_b[:, :]),
                           e.dma_start(out=sel, in_=sel_b[:, :]),
                           e.dma_start(out=S_f[:, 0:8, :], in_=S_all[0][0:1024, :].rearrange("(c p) v -> p c v", p=128)),
                           e.dma_start(out=S_f[:, 8:16, :], in_=S_all[1][0:1024, :].rearrange("(c p) v -> p c v", p=128))],
          writes=["r3c"] + [("S_f", c) for c in range(16)], dma_key="k_r3c", ndma=7)
    for c in range(16):
        P.add("dve" if c % 2 == 0 else POOLC, lambda e, c=c: e.tensor_scalar(S_f[:, c, :], S_f[:, c, :], sel[:, 0:1], None, ALU.mult),
              reads=["r3c", ("S_f", c)], writes=[("S_f", c)])
        P.add("act", lambda e, c=c: e.activation(S_b[:, c, :], S_f[:, c, :], AF.Identity), reads=[("S_f", c)], writes=[("S_b", c)])
    NSC = DEBUG.get('r3n', 16)

    def r3_load(n):
        b_ = n % 2
        t0_ = n * 128
        P.add("sp", lambda e, b_=b_, t0_=t0_: [
            e.dma_start(out=qn[b_], in_=QTd.ap().rearrange("(c p) t -> p c t", p=128)[:, :, t0_:t0_ + 128]),
            e.dma_start(out=kn[b_], in_=KTd.ap().rearrange("(c p) t -> p c t", p=128)[:, :, t0_:t0_ + 128]),
            e.dma_start(out=ktn[b_], in_=Ktd[t0_:t0_ + 128, :]),
            e.dma_start(out=vn[b_], in_=Vd[t0_:t0_ + 128, :]),
            e.dma_start(out=gn[b_], in_=Gd[t0_:t0_ + 128, :])],
            writes=[("r3in", b_)], dma_key=("k_r3l", b_), ndma=5)

    r3_load(0)
    for n in range(NSC):
        b_ = n % 2
        t0_ = n * 128
        if n + 1 < NSC:
            r3_load(n + 1)
        for h in range(DEBUG.get('r3h', RET_H)):
            p_ = h % 2
            gam = 1.0 - 2.0 ** (-5.0 - h)
            g128 = float(gam ** 128)
            psA = ps[0][:, p_ * 128:(p_ + 1) * 128]
            p_oi = ps[1 + p_]
            p_oc = ps[3 + p_]

            def mmA(e, b_=b_, h=h, psA=psA):
                r = None
                for dkt in range(2):
                    r = e.matmul(psA, lhsT=kn[b_][:, 2 * h + dkt, :], rhs=qn[b_][:, 2 * h + dkt, :],
                                 start=(dkt == 0), stop=(dkt == 1))
                return r
            P.add("pe", mmA, reads=[("r3in", b_)], writes=[("psA", p_)])
            P.add("dve", lambda e, p_=p_, h=h, psA=psA: e.tensor_tensor(at_sb[:, p_, :], psA, dmask[:, h, :], ALU.mult),
                  reads=[("psA", p_), "r3c"], writes=[("at", p_)])
            P.add("pe", lambda e, p_=p_, b_=b_, h=h, p_oi=p_oi: e.matmul(p_oi[:, :], lhsT=at_sb[:, p_, :],
                                                                         rhs=vn[b_][:, h * 512:(h + 1) * 512], start=True, stop=True),
                  reads=[("at", p_), ("r3in", b_)], writes=[("ps", 1 + p_)])

            def mmC(e, b_=b_, h=h, p_oc=p_oc):
                r = None
                for dkt in range(2):
                    r = e.matmul(p_oc[:, :], lhsT=qn[b_][:, 2 * h + dkt, :], rhs=S_b[:, 2 * h + dkt, :],
                                 start=(dkt == 0), stop=(dkt == 1))
                return r
            P.add("pe", mmC, reads=[("r3in", b_), ("S_b", 2 * h), ("S_b", 2 * h + 1)], writes=[("ps", 3 + p_)])
            P.add("act", lambda e, p_=p_, p_oi=p_oi: e.activation(oi_sb[:, p_, :], p_oi[:, :], AF.Identity),
                  reads=[("ps", 1 + p_)], writes=[("oi", p_)])
            P.add("dve", lambda e, p_=p_, h=h, p_oc=p_oc: e.scalar_tensor_tensor(o_sb[:, p_, :], p_oc[:, :], qdec[:, h:h + 1],
                                                                                oi_sb[:, p_, :], ALU.mult, ALU.add),
                  reads=[("ps", 3 + p_), ("oi", p_), "r3c"], writes=[("o", p_)])
            if 'state' in DEBUG.get('r3skip', ()):
                continue
            P.add(POOLC, lambda e, p_=p_, b_=b_, h=h: e.tensor_scalar(kd_sb[:, p_, :], ktn[b_][:, h * 256:(h + 1) * 256],
                                                                       kdec[:, h:h + 1], None, ALU.mult),
                  reads=[("r3in", b_), "r3c"], writes=[("kd", p_)])
            for dkt in range(2):
                bank = 5 + dkt
                c = 2 * h + dkt
                P.add("pe", lambda e, p_=p_, b_=b_, h=h, dkt=dkt, bank=bank: e.matmul(
                    ps[bank][:, :], lhsT=kd_sb[:, p_, dkt * 128:(dkt + 1) * 128], rhs=vn[b_][:, h * 512:(h + 1) * 512],
                    start=True, stop=True),
                    reads=[("kd", p_), ("r3in", b_)], writes=[("ps", bank)])
                P.add("dve", lambda e, c=c, bank=bank, g128=g128: e.scalar_tensor_tensor(
                    S_f[:, c, :], S_f[:, c, :], g128, ps[bank][:, :], ALU.mult, ALU.add),
                    reads=[("ps", bank), ("S_f", c)], writes=[("S_f", c)])
                P.add("act", lambda e, c=c: e.activation(S_b[:, c, :], S_f[:, c, :], AF.Identity),
                      reads=[("S_f", c)], writes=[("S_b", c)])
            if 'ln' in DEBUG.get('r3skip', ()):
                continue
            sv = st[:, p_, :]
            P.add("dve", lambda e, p_=p_, sv=sv: e.reduce_sum(sv[:, 0:1], o_sb[:, p_, :], axis=AX.X),
                  reads=[("o", p_)], writes=[("st", p_, 0)])
            P.add(POOLC, lambda e, p_=p_: e.tensor_tensor(osq[:, p_, :], o_sb[:, p_, :], o_sb[:, p_, :], ALU.mult),
                  reads=[("o", p_)], writes=[("osq", p_)])
            P.add("dve", lambda e, p_=p_, sv=sv: e.reduce_sum(sv[:, 1:2], osq[:, p_, :], axis=AX.X),
                  reads=[("osq", p_)], writes=[("st", p_, 1)])
            P.add("dve", lambda e, sv=sv: e.tensor_scalar(sv[:, 2:3], sv[:, 0:1], -1.0 / 512, None, ALU.mult),
                  reads=[("st", p_, 0)], writes=[("st", p_, 2)])
            P.add("dve", lambda e, sv=sv: e.tensor_tensor(sv[:, 3:4], sv[:, 2:3], sv[:, 2:3], ALU.mult),
                  reads=[("st", p_, 2)], writes=[("st", p_, 3)])
            P.add("dve", lambda e, sv=sv: e.scalar_tensor_tensor(sv[:, 4:5], sv[:, 1:2], 1.0 / 512, sv[:, 3:4], ALU.mult, ALU.subtract),
                  reads=[("st", p_, 1), ("st", p_, 3)], writes=[("st", p_, 4)])
            P.add("act", lambda e, sv=sv: e.activation(sv[:, 4:5], sv[:, 4:5], AF.Sqrt, bias=float(EPS), scale=1.0),
                  reads=[("st", p_, 4)], writes=[("st", p_, 4)])
            P.add("dve", lambda e, sv=sv: e.reciprocal(sv[:, 4:5], sv[:, 4:5]), reads=[("st", p_, 4)], writes=[("st", p_, 4)])
            P.add("dve", lambda e, sv=sv: e.tensor_tensor(sv[:, 5:6], sv[:, 2:3], sv[:, 4:5], ALU.mult),
                  reads=[("st", p_, 2), ("st", p_, 4)], writes=[("st", p_, 5)])
            P.add("act", lambda e, p_=p_, sv=sv: e.activation(y_sb[:, p_, :], o_sb[:, p_, :], AF.Identity, bias=sv[:, 5:6], scale=sv[:, 4:5]),
                  reads=[("o", p_), ("st", p_, 5), ("st", p_, 4)], writes=[("y", p_)])
            P.add(POOLC, lambda e, p_=p_, b_=b_, h=h: e.tensor_tensor(ogn[b_][:, h * 512:(h + 1) * 512], y_sb[:, p_, :],
                                                                       gn[b_][:, h * 512:(h + 1) * 512], ALU.mult),
                  reads=[("y", p_), ("r3in", b_)], writes=[("ogn", b_, h)])
        P.add("sp", lambda e, b_=b_, t0_=t0_: [e.dma_start(out=OGd[t0_:t0_ + 128, :], in_=ogn[b_])],
              reads=[("ogn", b_, h) for h in range(DEBUG.get('r3h', RET_H))], writes=[("OGd", n)], dma_key=("k_r3s", b_))


def ret4_load(B, xT_d, OGd, ret_w_o, t, slot):
    P = B.P
    ps = B.ps
    B.load_x_scratch(xT_d, t, slot)
    OGT = B.view(B.actT_off, [128, 32, T], BF16)
    stg = B.view(B.actT_off + 32 * T * 2, [128, 4096], BF16)
    stg_res = [("actT", f) for f in range(32, 40)]
    for u in range(4):
        r0 = t * T + u * 128
        P.add("sp", lambda e, r0=r0: [e.dma_start(out=stg, in_=OGd[r0:r0 + 128, :])], writes=stg_res, dma_key="k_r4l")
        for g in range(8):
            bank = 6 + g % 2
            psb = ps[bank][:, 0:256].bitcast(BF16)

            def tr(e, g=g, psb=psb):
                r = None
                for j in range(4):
                    et = g * 4 + j
                    r = e.transpose(psb[:, j * 128:(j + 1) * 128], stg[:, et * 128:(et + 1) * 128], B.ident_b)
                return r
            P.add("pe", tr, reads=stg_res + ["ident_b"], writes=[("ps", bank)])
            dst = OGT[:, g * 4:(g + 1) * 4, u * 128:(u + 1) * 128]
            srcp = psb.rearrange("p (a b) -> p a b", a=4)
            if g % 2 == 0:
                P.add("dve", lambda e, dst=dst, srcp=srcp: e.tensor_copy(dst, srcp), reads=[("ps", bank)],
                      writes=[("actT", g * 4 + j) for j in range(4)])
            else:
                P.add("act", lambda e, dst=dst, srcp=srcp: e.activation(dst, srcp, AF.Identity), reads=[("ps", bank)],
                      writes=[("actT", g * 4 + j) for j in range(4)])
    mixer_out_proj(B, ret_w_o, 32, OGT, [("actT", f) for f in range(32)], slot)


def final_store(B, out_d, fg_pm_sb, t, slot):
    P = B.P
    ps = B.ps
    xt = B.xt[slot]
    pst = ps[6]
    for kt in range(NKT):
        q = kt % 4
        P.add("act", lambda e, kt=kt, q=q: e.activation(B.sq[:, q, :], xt[:, kt, :], AF.Square),
              reads=[("xt", slot, kt)], writes=[("sq", q)])
        P.add("pe", lambda e, kt=kt, q=q: e.matmul(pst[:, :], lhsT=B.ones_b, rhs=B.sq[:, q, :],
                                                   start=(kt == 0), stop=(kt == NKT - 1)),
              reads=[("sq", q), "ones_b"], writes=[("ps", 6)])
    P.add("act", lambda e: e.activation(B.rstd, pst[:, :], AF.Sqrt, bias=float(D * EPS), scale=1.0),
          reads=[("ps", 6)], writes=["rstd"])
    P.add("dve", lambda e: e.reciprocal(B.rstd, B.rstd), reads=["rstd"], writes=["rstd"])
    for kt in range(NKT):
        q = kt % 2
        P.add("dve", lambda e, kt=kt, q=q: e.tensor_tensor(B.tmp[:, q, :], xt[:, kt, :], B.rstd, ALU.mult),
              reads=[("xt", slot, kt), "rstd"], writes=[("tmp", q)])
        P.add("act", lambda e, kt=kt, q=q: e.activation(xt[:, kt, :], B.tmp[:, q, :], AF.Identity, scale=fg_pm_sb[:, kt:kt + 1]),
              reads=[("tmp", q), "fg"], writes=[("xt", slot, kt)])
    stg = B.view(B.actT_off + 32 * T * 2, [128, D], F32)
    stg_res = [("actT", f) for f in range(32, 40)]
    for u in range(4):
        for g in range(4):
            bank = 4 + g % 2

            def tr(e, g=g, u=u, bank=bank):
                r = None
                for j in range(4):
                    kt = g * 4 + j
                    r = e.transpose(ps[bank][:, j * 128:(j + 1) * 128], xt[:, kt, u * 128:(u + 1) * 128], B.ident_f)
                return r
            P.add("pe", tr, reads=[("xt", slot, g * 4 + j) for j in range(4)] + ["ident_f"], writes=[("ps", bank)])
            if g % 2 == 0:
                P.add("dve", lambda e, g=g, bank=bank: e.tensor_copy(stg[:, g * 512:(g + 1) * 512], ps[bank][:, :]),
                      reads=[("ps", bank)], writes=[("actT", 32 + 2 * g), ("actT", 33 + 2 * g)])
            else:
                P.add("act", lambda e, g=g, bank=bank: e.activation(stg[:, g * 512:(g + 1) * 512], ps[bank][:, :], AF.Identity),
                      reads=[("ps", bank)], writes=[("actT", 32 + 2 * g), ("actT", 33 + 2 * g)])
        r0 = t * T + u * 128
        P.add("sp", lambda e, r0=r0: [e.dma_start(out=out_d[r0:r0 + 128, :], in_=stg)],
              reads=stg_res, writes=[("outd", t, u)], dma_key="k_out")


def _pm(v):
    return np.ascontiguousarray(np.asarray(v, np.float32).reshape(16, 128).T)


def _da_consts(r):
    bias = np.zeros((4, 3, 128, QT), np.float32)
    offs = np.zeros((128, 128), np.float32)
    kr = np.arange(128)[:, None]
    qr = np.arange(QT)[None, :]
    for hh in range(4):
        h = 4 * r + hh
        slope = 2.0 ** (-8.0 * (h + 1) / 8)
        bias[hh, 0] = -slope * (qr - kr)
        for d in range(2):
            allowed = ((128 * d + kr) // 64) <= (qr // 64)
            bias[hh, 1 + d] = np.where(allowed, -slope * np.abs(qr - kr - 128 * d), NEG)
        for n in range(32):
            offs[:, hh * 32 + n] = -slope * 128 * n
    return bias, offs


def _ret_consts():
    gam = np.array([1.0 - 2.0 ** (-5.0 - h) for h in range(RET_H)], np.float64)
    p = np.arange(128)
    k = p[:, None, None]
    q = p[None, None, :]
    dm = gam[None, :, None] ** np.abs(q - k) * ((k // 64) <= (q // 64))
    qdec = gam[None, :] ** p[:, None]
    kdec = gam[None, :] ** (128 - p[:, None])
    st = np.arange(16)
    kg = gam[None, :, None] ** (TOK - (st[None, None, :] * 128 + p[:, None, None]))
    return (dm.astype(np.float32), qdec.astype(np.float32), kdec.astype(np.float32), kg.astype(np.float32))


_PROGS = {}
CC_UNIT = 1
POOLC = 'dve'
DEBUG = {}


def _prog(stages):
    key = tuple(stages)
    if key not in _PROGS:
        _PROGS[key] = build_program(list(stages))
    return _PROGS[key]


def _core_maps(inputs):
    f = lambda a: np.ascontiguousarray(np.asarray(a, np.float32))
    x = np.asarray(inputs["x"], np.float32)
    dm, qdec, kdec, kg = _ret_consts()
    ada_b_pm = np.ascontiguousarray(np.asarray(inputs["ada_b"], np.float32).reshape(2, 144, 128).transpose(0, 2, 1))
    norm_g_pm = np.ascontiguousarray(np.asarray(inputs["norm_g"], np.float32).reshape(2, 3, 16, 128).transpose(0, 3, 1, 2))
    w = np.asarray(inputs["da_w_qkv"], np.float32)[0]
    shared = {
        "c_ident": np.eye(128, dtype=np.float32),
        "ada_w_0": f(inputs["ada_w"][0]), "ada_w_1": f(inputs["ada_w"][1]),
        "ada_b_pm": ada_b_pm, "norm_g_pm": norm_g_pm,
        "ffn_w_in_00": f(inputs["ffn_w_in"][0, 0]), "ffn_w_in_01": f(inputs["ffn_w_in"][0, 1]),
        "ffn_w_in_10": f(inputs["ffn_w_in"][1, 0]), "ffn_w_in_11": f(inputs["ffn_w_in"][1, 1]),
        "ffn_w_out_00": f(inputs["ffn_w_out"][0, 0]), "ffn_w_out_01": f(inputs["ffn_w_out"][0, 1]),
        "ffn_w_out_10": f(inputs["ffn_w_out"][1, 0]), "ffn_w_out_11": f(inputs["ffn_w_out"][1, 1]),
        "da_lam_b": np.ascontiguousarray(np.broadcast_to(np.asarray(inputs["da_lambda"], np.float32)[0].reshape(1, 512), (128, 512))),
        "da_gvec_b": np.ascontiguousarray(np.broadcast_to(np.asarray(inputs["da_subln_g"], np.float32)[0][None], (128, 256))),
        "da_w_o": f(inputs["da_w_o"][0]),
        "ret_w_qkvg": f(inputs["ret_w_qkvg"][0]),
        "ret_kdec_glob": kg, "ret_dmask": dm, "ret_qdec": qdec, "ret_kdec": kdec,
        "ret_gng_b": np.ascontiguousarray(np.broadcast_to(np.asarray(inputs["ret_gn_g"], np.float32)[0][None], (128, 512))),
        "ret_w_o": f(inputs["ret_w_o"][0]),
        "final_g_pm": _pm(inputs["final_g"]),
    }
    da_my = []
    for r in range(2):
        hs = range(4 * r, 4 * r + 4)
        wq = np.concatenate([w[:, h * 256:(h + 1) * 256] for h in hs], 1)
        wk = np.concatenate([w[:, 2048 + h * 256:2048 + (h + 1) * 256] for h in hs], 1)
        wv = np.concatenate([w[:, 4096 + h * 256:4096 + (h + 1) * 256] for h in hs], 1)
        bias, offs = _da_consts(r)
        da_my.append({"da_wqkv_my": np.ascontiguousarray(np.concatenate([wq, wk, wv], 1)), "da_bias": bias, "da_offs": offs,
                      "sel_b": np.full((128, 1), float(r), np.float32)})
    maps = []
    for c in range(NCORES):
        b, r = c // 2, c % 2
        m = dict(shared)
        m.update(da_my[r])
        m["c_pm"] = _pm(inputs["c"][b])
        m["x_in"] = np.ascontiguousarray(x[b, r * TOK:(r + 1) * TOK])
        maps.append(m)
    return maps


def _input_names(nc):
    names = []
    for alloc in nc.allocations:
        if isinstance(alloc, mybir.MemoryLocationSet) and alloc.kind == "ExternalInput":
            names.append(alloc.memorylocations[0].name)
    return names


def _run(stages, maps, cores=None):
    nc = _prog(stages)
    names = set(_input_names(nc))
    cores = list(range(NCORES)) if cores is None else cores
    in_maps = [{k: v for k, v in maps[c].items() if k in names} for c in cores]
    res = run_bass_kernel_spmd(nc, in_maps, core_ids=list(range(len(cores))))
    return res.results


FUSED = True


def kernel(**inputs):
    maps = _core_maps(inputs)
    if not FUSED:
        r0 = _run([0], maps)
        for c in range(NCORES):
            b = c // 2
            maps[c]["xT0"] = r0[c]["xT0"]
            for t in range(NT):
                maps[c]["hT_all_%d" % t] = np.concatenate([r0[2 * b]["hT_own_%d" % t], r0[2 * b + 1]["hT_own_%d" % t]], 0)
        r1 = _run([1], maps)
        for c in range(NCORES):
            b = c // 2
            for k in range(4):
                maps[c]["O_all_%d" % k] = np.concatenate([r1[2 * b]["O_heads_%d" % k], r1[2 * b + 1]["O_heads_%d" % k]], 0)
        r2 = _run([2], maps)
        for c in range(NCORES):
            b = c // 2
            for k in ("xTb", "ret_QT", "ret_KT", "ret_Kt", "ret_V", "ret_G"):
                maps[c][k] = r2[c][k]
            for k in range(2):
                maps[c]["S_all_%d" % k] = np.concatenate([r2[2 * b]["S_loc_%d" % k], r2[2 * b + 1]["S_loc_%d" % k]], 0)
        r3 = _run([3], maps)
        outs = r3
    else:
        outs = _run([0, 1, 2, 3], maps)
    out = np.zeros((4, SEQ, D), np.float32)
    for c in range(NCORES):
        b, r = c // 2, c % 2
        out[b, r * TOK:(r + 1) * TOK] = outs[c]["out"]
    return out
```

```python
import math
import contextlib
import numpy as np
import ml_dtypes
import concourse.bass as bass
import concourse.mybir as mybir
from concourse.bass_utils import run_bass_kernel_spmd

F32 = mybir.dt.float32
BF16 = mybir.dt.bfloat16
AF = mybir.ActivationFunctionType
ALU = mybir.AluOpType
AX = mybir.AxisListType

D = 2048
FF = 5632
NKT = 16
NFC = 44
T = 512
TOK = 2048
NT = TOK // T
SEQ = 4096
EPS = 1e-5
NCORES = 8
DA_H = 8
RET_H = 8
NEG = -30000.0


class Op:
    __slots__ = ("eng", "fn", "deps", "need_inc", "cum", "key", "val", "waits", "is_dma", "ndma", "unit")


class Prog:
    ENG = ("pe", "act", "dve", "pool", "sp")

    def __init__(self):
        self.ops = []
        self.by_eng = {e: [] for e in self.ENG}
        self.lastw = {}
        self.readers = {}
        self.keycnt = {}
        self.final_keys = []

    def add(self, eng, fn, reads=(), writes=(), dma_key=None, ndma=1, unit=16):
        op = Op()
        op.unit = unit
        op.eng = eng
        op.fn = fn
        op.need_inc = False
        op.is_dma = dma_key is not None
        op.ndma = ndma
        op.cum = 0
        op.waits = []
        deps = []
        seen = set()

        def push(p):
            if p is None or p is op or id(p) in seen:
                return
            if (not p.is_dma) and p.eng == eng and eng == "pe":
                return
            seen.add(id(p))
            deps.append(p)

        for r in reads:
            push(self.lastw.get(r))
        for w in writes:
            push(self.lastw.get(w))
            for rd in self.readers.get(w, ()):
                push(rd)
        for r in reads:
            self.readers.setdefault(r, []).append(op)
        for w in writes:
            self.lastw[w] = op
            self.readers[w] = []
        if op.is_dma:
            op.key = dma_key
            c = self.keycnt.get(dma_key, 0) + unit * ndma
            self.keycnt[dma_key] = c
            op.val = c
        else:
            op.key = eng
            op.val = None
        op.deps = deps
        for p in deps:
            p.need_inc = True
        self.ops.append(op)
        self.by_eng[eng].append(op)
        return op

    def barrier(self):
        lasts = []
        for e in self.ENG:
            for op in reversed(self.by_eng[e]):
                if (not op.is_dma) and op.fn is not None:
                    lasts.append(op)
                    break
        dma_state = dict(self.keycnt)
        self._barriers = getattr(self, "_barriers", [])
        for e in self.ENG:
            op = self.add(e, None)
            for p in lasts:
                if p.eng != e:
                    op.deps.append(p)
                    p.need_inc = True
            op.waits = [(k, v) for k, v in dma_state.items()]

    def finalize(self):
        for e in self.ENG:
            c = 0
            for op in self.by_eng[e]:
                if (not op.is_dma) and op.need_inc:
                    c += 1
                op.cum = c
        waited = {e: {} for e in self.ENG}
        for op in self.ops:
            wd = waited[op.eng]
            out = []
            pre = op.waits
            for (k, v) in pre:
                if wd.get(k, 0) < v:
                    wd[k] = v
                    out.append((k, v))
            for p in op.deps:
                k = p.key
                v = p.val if p.is_dma else p.cum
                if wd.get(k, 0) < v:
                    wd[k] = v
                    out.append((k, v))
            op.waits = out

    def emit(self, nc, block, stack):
        self.finalize()
        keys = list(self.ENG) + list(self.keycnt.keys())
        sems = {}
        for i, k in enumerate(keys):
            sems[k] = stack.enter_context(nc.semaphore("s%d" % i))
        engmap = {"pe": block.tensor, "act": block.scalar, "dve": block.vector,
                  "pool": block.gpsimd, "sp": block.sync}
        final = [(k, self.keycnt[k]) for k in self.final_keys if k in self.keycnt]

        for e in self.ENG:
            ops = self.by_eng[e]

            def body(eng, ops=ops, e=e):
                for op in ops:
                    for (k, v) in op.waits:
                        eng.wait_ge(sems[k], v)
                    if op.fn is None:
                        continue
                    r = op.fn(eng)
                    if op.is_dma:
                        if not isinstance(r, (list, tuple)):
                            r = [r]
                        assert len(r) == op.ndma, (len(r), op.ndma)
                        for ins in r:
                            ins.then_inc(sems[op.key], op.unit)
                    elif op.need_inc:
                        r.then_inc(sems[op.key], 1)
                if e == "sp":
                    for (k, v) in final:
                        eng.wait_ge(sems[k], v)

            engmap[e](body)


class Builder:
    def __init__(self, stage_set):
        self.stages = stage_set
        self.nc = bass.Bass("TRN2", target_bir_lowering=False)
        self.P = Prog()
        self.dram = {}
        self.off = 0

    def din(self, name, shape, dt=F32):
        t = self.nc.dram_tensor(name, list(shape), dt, kind="ExternalInput")
        self.dram[name] = t
        return t

    def dout(self, name, shape, dt=F32):
        t = self.nc.dram_tensor(name, list(shape), dt, kind="ExternalOutput")
        self.dram[name] = t
        return t

    def dscr(self, name, shape, dt, kind):
        if kind == "in":
            return self.din(name, shape, dt)
        if kind == "out":
            return self.dout(name, shape, dt)
        t = self.nc.dram_tensor(name, list(shape), dt)
        self.dram[name] = t
        return t

    def carve(self, nbytes):
        o = self.off
        self.off += (nbytes + 63) // 64 * 64
        assert self.off <= self.arena_bytes, (self.off, self.arena_bytes)
        return o

    def view(self, off, shape, dt):
        esz = 4 if dt == F32 else 2
        n = 1
        for s in shape[1:]:
            n *= s
        a = self.arena[:, off // 2: off // 2 + n * esz // 2]
        if dt == F32:
            a = a.bitcast(F32)
        if len(shape) == 2:
            return a
        if len(shape) == 3:
            return a.rearrange("p (a b) -> p a b", a=shape[1])
        if len(shape) == 4:
            return a.rearrange("p (a b c) -> p a b c", a=shape[1], b=shape[2])
        raise ValueError

    def alloc(self, shape, dt):
        esz = 4 if dt == F32 else 2
        n = 1
        for s in shape[1:]:
            n *= s
        return self.view(self.carve(n * esz), shape, dt)

    def setup(self, stack):
        nc = self.nc
        self.arena_bytes = 204 * 1024
        self.arena = stack.enter_context(nc.sbuf_tensor("arena", [128, self.arena_bytes // 2], BF16))
        self.ps = [stack.enter_context(nc.psum_tensor("ps%d" % i, [128, 512], F32)) for i in range(8)]
        P = self.P
        self.ident_f = self.alloc([128, 128], F32)
        self.ident_b = self.alloc([128, 128], BF16)
        self.ones_b = self.alloc([128, 128], BF16)
        self.cst = self.alloc([128, 64], F32)
        c_ident = self.din("c_ident", [128, 128], F32)
        P.add("sp", lambda e: [e.dma_start(out=self.ident_f, in_=c_ident[:, :])], writes=["ident_f"], dma_key="k_c0")
        P.add("dve", lambda e: e.tensor_copy(self.ident_b, self.ident_f), reads=["ident_f"], writes=["ident_b"])
        P.add("dve", lambda e: e.memset(self.ones_b, 1.0), writes=["ones_b"])
        self.phase_base = self.off
        self.xt = [self.alloc([128, NKT, T], F32) for _ in range(2)]
        self.hT = self.alloc([128, NKT, T], BF16)
        self.actT_off = self.carve(NFC * T * 2)
        self.actT = self.view(self.actT_off, [128, NFC, T], BF16)
        self.w_off = [self.carve(22528) for _ in range(2)]
        self.sq = self.alloc([128, 4, T], BF16)
        self.tmp = self.alloc([128, 2, T], F32)
        self.sg = self.alloc([128, 2, T], F32)
        self.rstd = self.alloc([128, T], F32)
        self.modT = self.alloc([128, 144], F32)
        self.modv = self.alloc([128, 9, NKT], F32)
        self.adab = self.alloc([128, 144], F32)
        self.normg = self.alloc([128, 3, NKT], F32)
        self.cs_f = self.alloc([128, NKT], F32)
        self.cs_b = self.alloc([128, NKT], BF16)
        self.wctr = 0
        self.bgw = [self.alloc([128, NKT, 128], BF16) for _ in range(2)]

    bg = None
    bgctr = 0

    def mod_bg_start(self, li, parts, ada_w, ada_b_pm, norm_g_pm):
        P = self.P
        assert self.bg is None
        wsrc = ada_w[li].rearrange("(kt p) n -> p kt n", p=128)
        psm = self.ps[7]

        def gen():
            if getattr(self, "_mod_loaded", None) != li:
                self._mod_loaded = li
                P.add("sp", lambda e: [e.dma_start(out=self.adab, in_=ada_b_pm[li]),
                                       e.dma_start(out=self.normg, in_=norm_g_pm[li])],
                      writes=["adab", "normg"], dma_key="k_c2", ndma=2)
            pending = None
            for p_ in parts:
                for ct in range(48 * p_, 48 * p_ + 48):
                    sl = self.bgctr % 2
                    self.bgctr += 1
                    wv = self.bgw[sl]
                    P.add("pool", lambda e, wv=wv, ct=ct: [e.dma_start(out=wv, in_=wsrc[:, :, ct * 128:(ct + 1) * 128])],
                          writes=[("bgw", sl)], dma_key=("kbg", sl))

                    def mm(e, wv=wv, ct=ct):
                        r = None
                        for kt in range(NKT):
                            r = e.matmul(psm[:, ct:ct + 1], lhsT=wv[:, kt, :], rhs=self.cs_b[:, kt:kt + 1],
                                         start=(kt == 0), stop=(kt == NKT - 1))
                        return r
                    if pending is not None:
                        P.add("pe", pending[0], reads=[("bgw", pending[1]), "cs_b"], writes=[("ps", 7)])
                    pending = (mm, sl)
                    yield
                P.add("pe", pending[0], reads=[("bgw", pending[1]), "cs_b"], writes=[("ps", 7)])
                pending = None
                self._mod_derive(p_)
                yield
        self.bg = gen()

    def bg_step(self, n=1):
        for _ in range(n):
            if self.bg is None:
                return
            try:
                next(self.bg)
            except StopIteration:
                self.bg = None

    def bg_finish(self):
        while self.bg is not None:
            self.bg_step()

    def wview(self, slot, shape):
        return self.view(self.w_off[slot], shape, BF16)

    def next_w(self):
        s = self.wctr % 2
        self.wctr += 1
        return s

    def mod_layer(self, li, c_pm, ada_w, ada_b_pm, norm_g_pm, parts=(0, 1, 2)):
        P = self.P
        if not getattr(self, "_cs_done", False):
            self._cs_done = True
            P.add("sp", lambda e: [e.dma_start(out=self.cs_f, in_=c_pm[:, :])], writes=["cs_f"], dma_key="k_c1")
            P.add("act", lambda e: e.activation(self.cs_b, self.cs_f, AF.Silu), reads=["cs_f"], writes=["cs_b"])
        if getattr(self, "_mod_loaded", None) != li:
            self._mod_loaded = li
            P.add("sp", lambda e: [e.dma_start(out=self.adab, in_=ada_b_pm[li]),
                                   e.dma_start(out=self.normg, in_=norm_g_pm[li])],
                  writes=["adab", "normg"], dma_key="k_c2", ndma=2)
        wsrc = ada_w[li].rearrange("(kt p) n -> p kt n", p=128)
        psm = self.ps[7]
        chunks = [c_ for p_ in parts for c_ in range(12 * p_, 12 * p_ + 12)]
        for ch in chunks:
            s = self.next_w()
            wv = self.wview(s, [128, NKT, 512])
            P.add("pool", lambda e, wv=wv, ch=ch: [e.dma_start(out=wv, in_=wsrc[:, :, ch * 512:(ch + 1) * 512])],
                  writes=[("w", s)], dma_key=("kw", s))

            def mm(e, wv=wv, ch=ch):
                r = None
                for j in range(4):
                    ct = ch * 4 + j
                    for kt in range(NKT):
                        r = e.matmul(psm[:, ct:ct + 1], lhsT=wv[:, kt, j * 128:(j + 1) * 128],
                                     rhs=self.cs_b[:, kt:kt + 1], start=(kt == 0), stop=(kt == NKT - 1))
                return r
            P.add("pe", mm, reads=[("w", s), "cs_b"], writes=[("ps", 7)])
        for p_ in parts:
            self._mod_derive(p_)

    def _mod_derive(self, p_):
        P = self.P
        psm = self.ps[7]
        P.add("dve", lambda e, p_=p_: e.tensor_tensor(self.modT[:, 48 * p_:48 * p_ + 48], psm[:, 48 * p_:48 * p_ + 48],
                                                          self.adab[:, 48 * p_:48 * p_ + 48], ALU.add),
                  reads=[("ps", 7), "adab"], writes=[("modT", p_)])
        sqrtD = math.sqrt(float(D))
        for s3 in (p_,):
            gmul = 1.0 if s3 == 1 else 0.5

            sh = self.modT[:, (3 * s3) * 16:(3 * s3 + 1) * 16]
            sc = self.modT[:, (3 * s3 + 1) * 16:(3 * s3 + 2) * 16]
            gt = self.modT[:, (3 * s3 + 2) * 16:(3 * s3 + 3) * 16]
            A = self.modv[:, 3 * s3 + 0, :]
            B_ = self.modv[:, 3 * s3 + 1, :]
            G = self.modv[:, 3 * s3 + 2, :]
            P.add("dve", lambda e, A=A, sc=sc: e.tensor_scalar(A, sc, 1.0, sqrtD, ALU.add, ALU.mult),
                  reads=[("modT", s3)], writes=[("modv", s3, 0)])
            P.add("dve", lambda e, A=A, s3=s3: e.tensor_tensor(A, A, self.normg[:, s3, :], ALU.mult),
                  reads=[("modv", s3, 0), "normg"], writes=[("modv", s3, 0)])
            P.add("dve", lambda e, B_=B_, sh=sh: e.tensor_copy(B_, sh), reads=[("modT", s3)], writes=[("modv", s3, 1)])
            P.add("dve", lambda e, G=G, gt=gt, gmul=gmul: e.tensor_scalar(G, gt, gmul, None, ALU.mult),
                  reads=[("modT", s3)], writes=[("modv", s3, 2)])

    def load_x_scratch(self, xT_d, t, slot):
        src = xT_d.ap().rearrange("(kt p) t -> p kt t", p=128)[:, :, t * T:(t + 1) * T]
        self.P.add("sp", lambda e: [e.dma_start(out=self.xt[slot], in_=src)],
                   reads=[("xTd", t)], writes=[("xt", slot, k) for k in range(NKT)], dma_key=("kxl", slot))

    def store_x_scratch(self, xT_d, t, slot):
        dst = xT_d.ap().rearrange("(kt p) t -> p kt t", p=128)[:, :, t * T:(t + 1) * T]
        self.P.add("sp", lambda e: [e.dma_start(out=dst, in_=self.xt[slot])],
                   reads=[("xt", slot, k) for k in range(NKT)], writes=[("xTd", t)], dma_key=("kxs", slot))

    def load_x_transpose(self, x_in, t, slot):
        P = self.P
        xin = self.view(self.actT_off, [128, 2, D], F32)
        for u in range(4):
            us = u % 2
            r0 = t * T + u * 128
            P.add("sp", lambda e, us=us, r0=r0: [e.dma_start(out=xin[:, us, :], in_=x_in[r0:r0 + 128, :])],
                  writes=[("actT", f) for f in range(us * 8, us * 8 + 8)], dma_key=("kxin", us))
            for g in range(4):
                bank = 4 + (g % 2)

                def tr(e, us=us, g=g, bank=bank):
                    r = None
                    for j in range(4):
                        kt = g * 4 + j
                        r = e.transpose(self.ps[bank][:, j * 128:(j + 1) * 128], xin[:, us, kt * 128:(kt + 1) * 128],
                                        self.ident_f)
                    return r
                P.add("pe", tr, reads=[("actT", f) for f in range(us * 8, us * 8 + 8)] + ["ident_f"],
                      writes=[("ps", bank)])
                dst = self.xt[slot][:, g * 4:(g + 1) * 4, u * 128:(u + 1) * 128]
                srcp = self.ps[bank][:, :].rearrange("p (a b) -> p a b", a=4)
                eng = "dve" if g % 2 == 0 else "act"
                if eng == "dve":
                    P.add("dve", lambda e, dst=dst, srcp=srcp: e.tensor_copy(dst, srcp), reads=[("ps", bank)],
                          writes=[("xt", slot, g * 4 + j) for j in range(4)])
                else:
                    P.add("act", lambda e, dst=dst, srcp=srcp: e.activation(dst, srcp, AF.Identity),
                          reads=[("ps", bank)], writes=[("xt", slot, g * 4 + j) for j in range(4)])

    def norm_tile(self, slot, s3):
        P = self.P
        xt = self.xt[slot]
        A = self.modv[:, 3 * s3 + 0, :]
        B_ = self.modv[:, 3 * s3 + 1, :]
        pst = self.ps[6]
        for kt in range(NKT):
            q = kt % 4
            P.add("act", lambda e, kt=kt, q=q: e.activation(self.sq[:, q, :], xt[:, kt, :], AF.Square),
                  reads=[("xt", slot, kt)], writes=[("sq", q)])
            P.add("pe", lambda e, kt=kt, q=q: e.matmul(pst[:, :], lhsT=self.ones_b, rhs=self.sq[:, q, :],
                                                       start=(kt == 0), stop=(kt == NKT - 1)),
                  reads=[("sq", q), "ones_b"], writes=[("ps", 6)])
        P.add("act", lambda e: e.activation(self.rstd, pst[:, :], AF.Sqrt, bias=float(D * EPS), scale=1.0),
              reads=[("ps", 6)], writes=["rstd"])
        P.add("dve", lambda e: e.reciprocal(self.rstd, self.rstd), reads=["rstd"], writes=["rstd"])
        for kt in range(NKT):
            q = kt % 2
            P.add("dve", lambda e, kt=kt, q=q: e.tensor_tensor(self.tmp[:, q, :], xt[:, kt, :], self.rstd, ALU.mult),
                  reads=[("xt", slot, kt), "rstd"], writes=[("tmp", q)])
            P.add("act", lambda e, kt=kt, q=q: e.activation(self.hT[:, kt, :], self.tmp[:, q, :], AF.Identity,
                                                            bias=B_[:, kt:kt + 1], scale=A[:, kt:kt + 1]),
                  reads=[("tmp", q), ("modv", s3, 0), ("modv", s3, 1)], writes=[("hT", kt)])

    def ffn_sublayer(self, s3, w_in, w_out, load_fn, store_fn):
        P = self.P
        G = self.modv[:, 3 * s3 + 2, :]
        win = w_in.rearrange("(kt p) n -> p kt n", p=128)
        wout = w_out.rearrange("(fc p) d -> p fc d", p=128)
        fid = self.ffn_ctr = getattr(self, "ffn_ctr", -1) + 1
        wbf_a = self.nc.dram_tensor("wbf_a%d" % fid, [NFC // 2, 128, NKT * 512], BF16)
        wbf_b = self.nc.dram_tensor("wbf_b%d" % fid, [NKT // 2, 128, NFC * 256], BF16)
        for t in range(NT):
            slot = t % 2
            xt = self.xt[slot]
            load_fn(t, slot)
            self.norm_tile(slot, s3)
            for g in range(NFC // 2):
                s = self.next_w()
                wv = self.wview(s, [128, NKT, 512])
                wflat = self.wview(s, [128, NKT * 512])
                if t == 0:
                    P.add("pool", lambda e, wv=wv, g=g: [
                        e.dma_start(out=wv[:, :, 0:256], in_=win[:, :, g * 256:(g + 1) * 256]),
                        e.dma_start(out=wv[:, :, 256:512], in_=win[:, :, FF + g * 256:FF + (g + 1) * 256])],
                        writes=[("w", s)], dma_key=("kw", s), ndma=2)
                    P.add("sp", lambda e, wflat=wflat, g=g: [e.dma_start(out=wbf_a[g], in_=wflat)],
                          reads=[("w", s)], writes=[("wbfa", fid, g)], dma_key=("kws", s))
                else:
                    P.add("sp", lambda e, wflat=wflat, g=g: [e.dma_start(out=wflat, in_=wbf_a[g])],
                          reads=[("wbfa", fid, g)], writes=[("w", s)], dma_key=("kw", s))
                for j in range(2):
                    fc = 2 * g + j
                    pp = fc % 2
                    pa, pb = self.ps[2 * pp], self.ps[2 * pp + 1]

                    def mma(e, wv=wv, j=j, pa=pa, pb=pb):
                        r = None
                        for kt in range(NKT):
                            r = e.matmul(pa[:, :], lhsT=wv[:, kt, j * 128:(j + 1) * 128], rhs=self.hT[:, kt, :],
                                         start=(kt == 0), stop=(kt == NKT - 1))
                        for kt in range(NKT):
                            r = e.matmul(pb[:, :], lhsT=wv[:, kt, 256 + j * 128:256 + (j + 1) * 128],
                                         rhs=self.hT[:, kt, :], start=(kt == 0), stop=(kt == NKT - 1))
                        return r
                    P.add("pe", mma, reads=[("w", s)] + [("hT", k) for k in range(NKT)],
                          writes=[("ps", 2 * pp), ("ps", 2 * pp + 1)])
                    P.add("act", lambda e, pa=pa, pp=pp: e.activation(self.sg[:, pp, :], pa[:, :], AF.Silu),
                          reads=[("ps", 2 * pp)], writes=[("sg", pp)])
                    P.add("dve", lambda e, pb=pb, pp=pp, fc=fc: e.tensor_tensor(self.actT[:, fc, :], self.sg[:, pp, :],
                                                                               pb[:, :], ALU.mult),
                          reads=[("sg", pp), ("ps", 2 * pp + 1)], writes=[("actT", fc)])
                self.bg_step()
            for gd in range(NKT // 2):
                s = self.next_w()
                wv = self.wview(s, [128, NFC, 256])
                wflat = self.wview(s, [128, NFC * 256])
                if t == 0:
                    P.add("pool", lambda e, wv=wv, gd=gd: [e.dma_start(out=wv, in_=wout[:, :, gd * 256:(gd + 1) * 256])],
                          writes=[("w", s)], dma_key=("kw", s))
                    P.add("sp", lambda e, wflat=wflat, gd=gd: [e.dma_start(out=wbf_b[gd], in_=wflat)],
                          reads=[("w", s)], writes=[("wbfb", fid, gd)], dma_key=("kws", s))
                else:
                    P.add("sp", lambda e, wflat=wflat, gd=gd: [e.dma_start(out=wflat, in_=wbf_b[gd])],
                          reads=[("wbfb", fid, gd)], writes=[("w", s)], dma_key=("kw", s))
                for j in range(2):
                    dt_ = 2 * gd + j
                    py = self.ps[4 + dt_ % 2]

                    def mmb(e, wv=wv, j=j, py=py):
                        r = None
                        for fc in range(NFC):
                            r = e.matmul(py[:, :], lhsT=wv[:, fc, j * 128:(j + 1) * 128], rhs=self.actT[:, fc, :],
                                         start=(fc == 0), stop=(fc == NFC - 1))
                        return r
                    P.add("pe", mmb, reads=[("w", s)] + [("actT", f) for f in range(NFC)], writes=[("ps", 4 + dt_ % 2)])
                    P.add("dve", lambda e, py=py, dt_=dt_, xt=xt: e.scalar_tensor_tensor(
                        xt[:, dt_, :], py[:, :], G[:, dt_:dt_ + 1], xt[:, dt_, :], ALU.mult, ALU.add),
                        reads=[("ps", 4 + dt_ % 2), ("modv", s3, 2), ("xt", slot, dt_)], writes=[("xt", slot, dt_)])
                self.bg_step()
            store_fn(t, slot)
        self.bg_finish()

    after_hT = None

    def da_part1(self, xT_d, hT_d):
        P = self.P
        for t in range(NT):
            slot = t % 2
            self.load_x_scratch(xT_d, t, slot)
            self.norm_tile(slot, 1)
            dst = hT_d[t].ap().rearrange("(kt p) t -> p kt t", p=128)
            P.add("sp", lambda e, dst=dst: [e.dma_start(out=dst, in_=self.hT)],
                  reads=[("hT", k) for k in range(NKT)], writes=[("hTd", t)], dma_key=("khs", 0))
            if self.after_hT is not None:
                self.after_hT(t)


def build_program(stages):
    B = Builder(set(stages))
    nc = B.nc
    P = B.P
    st = B.stages

    def kind(prod, cons):
        if prod in st and all(c in st for c in cons):
            return "int"
        if prod in st:
            return "out"
        return "in"

    stack = contextlib.ExitStack()
    with stack:
        B.setup(stack)
        need_mod0 = (0 in st) or (2 in st)
        need_mod1 = (2 in st) or (3 in st)
        class _Lazy:
            def __init__(self, name, shape):
                self.name, self.shape, self.c = name, shape, {}

            def __getitem__(self, idx):
                if not isinstance(idx, tuple):
                    idx = (idx,)
                if idx not in self.c:
                    self.c[idx] = B.din(self.name + "_" + "".join(str(i) for i in idx), self.shape).ap()
                return self.c[idx]
        if need_mod0 or need_mod1:
            c_pm = B.din("c_pm", [128, NKT])
            ada_w = _Lazy("ada_w", [D, 9 * D])
            ada_b_pm = B.din("ada_b_pm", [2, 128, 144])
            norm_g_pm = B.din("norm_g_pm", [2, 128, 3, NKT])
        ffn_w_in = _Lazy("ffn_w_in", [D, 2 * FF])
        ffn_w_out = _Lazy("ffn_w_out", [FF, D])
        PAIRS = [[0, 1], [2, 3], [4, 5], [6, 7]]

        def all_gather(src, dst, src_res, name):
            P.add("pool", lambda e: [e.collective_compute("AllGather", ALU.bypass, replica_groups=PAIRS,
                                                          ins=[src.ap().opt()], outs=[dst.ap().opt()])],
                  reads=list(src_res) + ["cc_chain"], writes=[name, "cc_chain"], dma_key="k_" + name, unit=CC_UNIT)
        finals = []
        if 0 in st:
            x_in = B.din("x_in", [TOK, D])
            xT0 = B.dscr("xT0", [D, TOK], F32, kind(0, [2]))
            hT_own = [B.dscr("hT_own_%d" % t, [D, T], BF16, kind(0, [1])) for t in range(NT)]
            if 1 in st:
                hT_all = [B.dscr("hT_all_%d" % t, [2 * D, T], BF16, "int") for t in range(NT)]
                B.after_hT = lambda t: all_gather(hT_own[t], hT_all[t], [("hTd", t)], "cc_h%d" % t)
            B.mod_layer(0, c_pm, ada_w, ada_b_pm, norm_g_pm, parts=(0,))

            def st0(t, slot):
                B.store_x_scratch(xT0, t, slot)
            B.mod_bg_start(0, (1, 2), ada_w, ada_b_pm, norm_g_pm)
            B.ffn_sublayer(0, ffn_w_in[0, 0], ffn_w_out[0, 0],
                           lambda t, slot: B.load_x_transpose(x_in, t, slot), st0)
            B.da_part1(xT0, hT_own)
            if kind(0, [2]) == "out":
                finals += [("kxs", 0), ("kxs", 1), ("khs", 0)]
        if 1 in st:
            if 0 in st:
                P.barrier()
            else:
                hT_all = [B.dscr("hT_all_%d" % t, [2 * D, T], BF16, "in") for t in range(NT)]
            da_wqkv_my = B.din("da_wqkv_my", [D, 3072])
            da_bias = B.din("da_bias", [4, 3, 128, QT])
            da_offs = B.din("da_offs", [128, 128])
            da_lam_b = B.din("da_lam_b", [128, 512])
            da_gvec_b = B.din("da_gvec_b", [128, 256])
            O_heads = [B.dscr("O_heads_%d" % k, [1024, 1024], BF16, kind(1, [2])) for k in range(4)]
            da_part2(B, hT_all, da_wqkv_my, da_bias, da_offs, da_lam_b, da_gvec_b, O_heads)
            if kind(1, [2]) == "out":
                finals += [("k_oo", 0), ("k_oo", 1)]
        if 2 in st:
            sel_b = B.din("sel_b", [128, 1])
            O_all = [B.dscr("O_all_%d" % k, [2048, 1024], BF16, "int" if 1 in st else "in") for k in range(4)]
            if 1 in st:
                P.barrier()
                for k in range(4):
                    all_gather(O_heads[k], O_all[k], [], "cc_o%d" % k)
            if 0 not in st:
                xT0 = B.din("xT0", [D, TOK], F32)
                B.mod_layer(0, c_pm, ada_w, ada_b_pm, norm_g_pm)
            else:
                P.barrier()
            da_w_o = B.din("da_w_o", [D, D])
            ret_w = B.din("ret_w_qkvg", [D, 12288])
            kdec_glob = B.din("ret_kdec_glob", [128, 8, 16])
            xTa = B.dscr("xTa", [D, TOK], F32, "int")
            xTb = B.dscr("xTb", [D, TOK], F32, kind(2, [3]))
            k23 = kind(2, [3])
            QTd = B.dscr("ret_QT", [D, TOK], BF16, k23)
            KTd = B.dscr("ret_KT", [D, TOK], BF16, k23)
            Ktd = B.dscr("ret_Kt", [TOK, D], BF16, k23)
            Vd = B.dscr("ret_V", [TOK, 2 * D], BF16, k23)
            Gd = B.dscr("ret_G", [TOK, 2 * D], BF16, k23)
            S_loc = [B.dscr("S_loc_%d" % k, [1024, 512], F32, k23) for k in range(2)]
            da3_setup(B, sel_b)
            def st2a(t, slot):
                B.store_x_scratch(xTa, t, slot)

            def st2b(t, slot):
                B.store_x_scratch(xTb, t, slot)
            B.mod_bg_start(1, (0,), ada_w, ada_b_pm, norm_g_pm)
            B.ffn_sublayer(2, ffn_w_in[0, 1], ffn_w_out[0, 1],
                           lambda t, slot: da3_load(B, xT0, O_all, da_w_o[:, :], t, slot), st2a)
            B.mod_bg_start(1, (1, 2), ada_w, ada_b_pm, norm_g_pm)
            B.ffn_sublayer(0, ffn_w_in[1, 0], ffn_w_out[1, 0],
                           lambda t, slot: B.load_x_scratch(xTa, t, slot), st2b)
            ret_part1(B, xTb, ret_w[:, :], QTd, KTd, Ktd, Vd, Gd)
            P.barrier()
            ret_part2(B, Ktd, Vd, kdec_glob, S_loc)
            if k23 == "out":
                finals += [("kxs", 0), ("kxs", 1), ("k_rs", 0), ("k_rs", 1), ("k_r2s", 0), ("k_r2s", 1)]
        if 3 in st:
            if 2 not in st:
                sel_b = B.din("sel_b", [128, 1])
                xTb = B.din("xTb", [D, TOK], F32)
                QTd = B.din("ret_QT", [D, TOK], BF16)
                KTd = B.din("ret_KT", [D, TOK], BF16)
                Ktd = B.din("ret_Kt", [TOK, D], BF16)
                Vd = B.din("ret_V", [TOK, 2 * D], BF16)
                Gd = B.din("ret_G", [TOK, 2 * D], BF16)
                B.mod_layer(1, c_pm, ada_w, ada_b_pm, norm_g_pm)
            S_all = [B.dscr("S_all_%d" % k, [2048, 512], F32, "int" if 2 in st else "in") for k in range(2)]
            if 2 in st:
                P.barrier()
                for k in range(2):
                    all_gather(S_loc[k], S_all[k], [], "cc_s%d" % k)
            ret_dmask = B.din("ret_dmask", [128, 8, 128])
            ret_qdec = B.din("ret_qdec", [128, 8])
            ret_kdec = B.din("ret_kdec", [128, 8])
            ret_gng_b = B.din("ret_gng_b", [128, 512])
            ret_w_o = B.din("ret_w_o", [2 * D, D])
            fg_pm = B.din("final_g_pm", [128, NKT])
            out_d = B.dout("out", [TOK, D], F32)
            OGd = B.dscr("ret_OG", [TOK, 2 * D], BF16, "out" if DEBUG.get("r3only") else ("in" if DEBUG.get("r4only") else "int"))
            fg_sb = B.alloc([128, NKT], F32)
            P.add("sp", lambda e: [e.dma_start(out=fg_sb, in_=fg_pm[:, :])], writes=["fg"], dma_key="k_fg")
            P.add("dve", lambda e: e.tensor_scalar(fg_sb, fg_sb, math.sqrt(float(D)), None, ALU.mult), reads=["fg"], writes=["fg"])
            P.barrier()
            if not DEBUG.get("r4only"):
                ret_part3(B, QTd, KTd, Ktd, Vd, Gd, S_all, sel_b, ret_dmask, ret_qdec, ret_kdec, ret_gng_b, OGd)
                P.barrier()
            if DEBUG.get("r3only"):
                finals += [("k_r3s", 0), ("k_r3s", 1)]
            else:
                B.ffn_sublayer(2, ffn_w_in[1, 1], ffn_w_out[1, 1],
                               lambda t, slot: ret4_load(B, xTb, OGd, ret_w_o[:, :], t, slot),
                               lambda t, slot: final_store(B, out_d, fg_sb, t, slot))
                finals += ["k_out"]
        P.final_keys = finals
        with nc.allow_low_precision("bf16 matmul operands, fp32 accumulation"), \
                nc.allow_non_contiguous_dma("weight tile streaming"):
            with nc.Block() as block:
                P.emit(nc, block, stack)
    return nc


QT = 256
NQT = SEQ // QT
NKT_SEQ = SEQ // 128


def da_part2(B, hT_all, da_wqkv_my, da_bias, da_offs, da_lam_b, da_gvec_b, O_heads):
    P = B.P
    base = B.phase_base
    off = base

    def al(shape, dt):
        nonlocal off
        esz = 4 if dt == F32 else 2
        n = 1
        for s in shape[1:]:
            n *= s
        v = B.view(off, shape, dt)
        off += (n * esz + 63) // 64 * 64
        assert off <= B.arena_bytes, off
        return v
    hs = [al([128, NKT, T], BF16) for _ in range(2)]
    wqk = al([128, NKT, 512], BF16)
    wv = al([128, NKT, 256], BF16)
    qT = al([128, 2, SEQ], BF16)
    kT = al([128, 2, SEQ], BF16)
    V = al([128, NKT_SEQ, 264], BF16)
    bias = al([128, 3, 2 * QT], F32)
    s_sb = al([128, 4, 2 * QT], F32)
    pT = al([128, 4, 2 * QT], BF16)
    offs = al([128, 128], F32)
    lamb = al([128, 512], F32)
    lprod = al([128, 256], F32)
    lsum = al([128, 2], F32)
    neglam = al([128, 1], F32)
    gvec = al([128, 256], F32)
    rden = al([128, 4], F32)
    o_sb = al([128, 2, 256], F32)
    osq = al([128, 2, 256], F32)
    ssq = al([128, 2], F32)
    o_out = al([128, 2, 256], BF16)
    ps = B.ps
    lambda_init = 0.8 - 0.6 * math.exp(-0.3 * 0)

    P.add("sp", lambda e: [e.dma_start(out=offs, in_=da_offs[:, :]),
                           e.dma_start(out=lamb, in_=da_lam_b[:, :]),
                           e.dma_start(out=gvec, in_=da_gvec_b[:, :])],
          writes=["offs", "lamb", "gvec"], dma_key="k_dac", ndma=3)
    P.add("dve", lambda e: e.memset(V[:, :, 256:257], 1.0), writes=["Vones"])
    P.add("dve", lambda e: e.tensor_scalar(gvec, gvec, 1.0 - lambda_init, None, ALU.mult), reads=["gvec"], writes=["gvec"])
    P.add("dve", lambda e: e.tensor_tensor(lprod[:, 0:128], lamb[:, 0:128], lamb[:, 128:256], ALU.mult),
          reads=["lamb"], writes=["lprod0"])
    P.add("dve", lambda e: e.tensor_tensor(lprod[:, 128:256], lamb[:, 256:384], lamb[:, 384:512], ALU.mult),
          reads=["lamb"], writes=["lprod1"])
    P.add("dve", lambda e: e.reduce_sum(lsum[:, 0:1], lprod[:, 0:128], axis=AX.X), reads=["lprod0"], writes=["lsum0"])
    P.add("dve", lambda e: e.reduce_sum(lsum[:, 1:2], lprod[:, 128:256], axis=AX.X), reads=["lprod1"], writes=["lsum1"])
    P.add("act", lambda e: e.activation(lsum, lsum, AF.Exp), reads=["lsum0", "lsum1"], writes=["lsum0", "lsum1"])
    P.add("dve", lambda e: e.tensor_tensor(neglam, lsum[:, 1:2], lsum[:, 0:1], ALU.subtract),
          reads=["lsum0", "lsum1"], writes=["neglam"])
    P.add("dve", lambda e: e.tensor_scalar(neglam, neglam, -lambda_init, None, ALU.add), reads=["neglam"], writes=["neglam"])

    wsrc = da_wqkv_my.ap().rearrange("(kt p) n -> p kt n", p=128)
    qscale = 128 ** -0.5
    for hh in range(4):
        P.add("pool", lambda e, hh=hh: [
            e.dma_start(out=wqk[:, :, 0:256], in_=wsrc[:, :, hh * 256:(hh + 1) * 256]),
            e.dma_start(out=wqk[:, :, 256:512], in_=wsrc[:, :, 1024 + hh * 256:1024 + (hh + 1) * 256]),
            e.dma_start(out=wv, in_=wsrc[:, :, 2048 + hh * 256:2048 + (hh + 1) * 256])],
            writes=["wqk", "wv"], dma_key="k_daw", ndma=3)
        P.add("sp", lambda e, hh=hh: [e.dma_start(out=bias[:, :, 0:QT], in_=da_bias[hh].rearrange("a p q -> p a q")),
                                      e.dma_start(out=bias[:, :, QT:2 * QT], in_=da_bias[hh].rearrange("a p q -> p a q"))],
              writes=["dabias"], dma_key="k_dab", ndma=2)
        pjc = 0
        for tt in range(SEQ // T):
            hsl = tt % 2
            rho, lt = tt // NT, tt % NT
            src = hT_all[lt].ap()[rho * D:(rho + 1) * D, :].rearrange("(kt p) t -> p kt t", p=128)
            P.add("sp", lambda e, src=src, hsl=hsl: [e.dma_start(out=hs[hsl], in_=src)],
                  writes=[("hs", hsl)], dma_key=("k_hs", hsl))
            for m in range(4):
                bank = [0, 1, 6, 7][pjc % 4]
                pjc += 1

                def mmq(e, m=m, hsl=hsl, bank=bank):
                    r = None
                    for kt in range(NKT):
                        r = e.matmul(ps[bank][:, :], lhsT=wqk[:, kt, m * 128:(m + 1) * 128], rhs=hs[hsl][:, kt, :],
                                     start=(kt == 0), stop=(kt == NKT - 1))
                    return r
                P.add("pe", mmq, reads=["wqk", ("hs", hsl)], writes=[("ps", bank)])
                if m < 2:
                    P.add("act", lambda e, m=m, tt=tt, bank=bank: e.activation(
                        qT[:, m, tt * T:(tt + 1) * T], ps[bank][:, :], AF.Identity, scale=qscale),
                        reads=[("ps", bank)], writes=[("qT", tt)])
                else:
                    P.add("dve", lambda e, m=m, tt=tt, bank=bank: e.tensor_copy(
                        kT[:, m - 2, tt * T:(tt + 1) * T], ps[bank][:, :]),
                        reads=[("ps", bank)], writes=[("kT", tt)])
            for u2 in range(2):
                bank = [0, 1, 6, 7][pjc % 4]
                pjc += 1

                def mmv(e, u2=u2, hsl=hsl, bank=bank):
                    r = None
                    for uu in range(2):
                        u = u2 * 2 + uu
                        for kt in range(NKT):
                            r = e.matmul(ps[bank][:, uu * 256:(uu + 1) * 256], lhsT=hs[hsl][:, kt, u * 128:(u + 1) * 128],
                                         rhs=wv[:, kt, :], start=(kt == 0), stop=(kt == NKT - 1))
                    return r
                P.add("pe", mmv, reads=["wv", ("hs", hsl)], writes=[("ps", bank)])
                i0 = tt * 4 + u2 * 2
                eng = "act" if u2 == 0 else "dve"
                dstv = V[:, i0:i0 + 2, 0:256]
                srcv = ps[bank][:, :].rearrange("p (a b) -> p a b", a=2)
                if eng == "act":
                    P.add("act", lambda e, dstv=dstv, srcv=srcv: e.activation(dstv, srcv, AF.Identity),
                          reads=[("ps", bank)], writes=[("V", i0), ("V", i0 + 1)])
                else:
                    P.add("dve", lambda e, dstv=dstv, srcv=srcv: e.tensor_copy(dstv, srcv),
                          reads=[("ps", bank)], writes=[("V", i0), ("V", i0 + 1)])
        SB = [0, 1, 6, 7]
        LA = 3
        units = [(j, i) for j in range(NQT) for i in range(2 * (j + 1))]

        def emit_scores(idx, j, i):
            sl = idx % 4
            bank = SB[sl]

            def mms(e, i=i, j=j, bank=bank):
                r = None
                for m in range(2):
                    r = e.matmul(ps[bank][:, m * QT:(m + 1) * QT], lhsT=kT[:, m, i * 128:(i + 1) * 128],
                                 rhs=qT[:, m, j * QT:(j + 1) * QT], start=True, stop=True)
                return r
            P.add("pe", mms, reads=[("kT", i // 4), ("qT", j // 2)], writes=[("ps", bank)])
            bt = 0 if i < 2 * j else 1 + (i - 2 * j)
            P.add("dve", lambda e, sl=sl, bt=bt, bank=bank: e.tensor_tensor(s_sb[:, sl, :], ps[bank][:, :], bias[:, bt, :], ALU.add),
                  reads=[("ps", bank), "dabias"], writes=[("s_sb", sl)])
            if i < 2 * j:
                bcol = offs[:, hh * 32 + (2 * j - i):hh * 32 + (2 * j - i) + 1]
                P.add("act", lambda e, sl=sl, bcol=bcol: e.activation(pT[:, sl, :], s_sb[:, sl, :], AF.Exp, bias=bcol),
                      reads=[("s_sb", sl), "offs"], writes=[("pT", sl)])
            else:
                P.add("act", lambda e, sl=sl: e.activation(pT[:, sl, :], s_sb[:, sl, :], AF.Exp),
                      reads=[("s_sb", sl)], writes=[("pT", sl)])

        def emit_pv(idx, j, i):
            sl = idx % 4
            nk = 2 * (j + 1)

            def pv(e, sl=sl, i=i, nk=nk):
                r = None
                for m in range(2):
                    for u in range(2):
                        r = e.matmul(ps[2 + m * 2 + u][:, 0:257], lhsT=pT[:, sl, m * QT + u * 128:m * QT + (u + 1) * 128],
                                     rhs=V[:, i, 0:257], start=(i == 0), stop=(i == nk - 1))
                return r
            P.add("pe", pv, reads=[("pT", sl), ("V", i), "Vones"], writes=[("acc", m, u) for m in range(2) for u in range(2)])
            if i == nk - 1:
                epilogue(j)

        def epilogue(j):
            for u in range(2):
                a0 = ps[2 + u]
                a1 = ps[4 + u]
                P.add("dve", lambda e, a0=a0, u=u: e.reciprocal(rden[:, 2 * u:2 * u + 1], a0[:, 256:257]),
                      reads=[("acc", 0, u)], writes=[("rden", u, 0)])
                P.add("dve", lambda e, a1=a1, u=u: e.reciprocal(rden[:, 2 * u + 1:2 * u + 2], a1[:, 256:257]),
                      reads=[("acc", 1, u)], writes=[("rden", u, 1)])
                P.add("dve", lambda e, u=u: e.tensor_tensor(rden[:, 2 * u + 1:2 * u + 2], rden[:, 2 * u + 1:2 * u + 2], neglam, ALU.mult),
                      reads=[("rden", u, 1), "neglam"], writes=[("rden", u, 1)])
                P.add("dve", lambda e, a0=a0, u=u: e.tensor_scalar(o_sb[:, u, :], a0[:, 0:256], rden[:, 2 * u:2 * u + 1], None, ALU.mult),
                      reads=[("acc", 0, u), ("rden", u, 0)], writes=[("o_sb", u)])
                P.add("dve", lambda e, a1=a1, u=u: e.scalar_tensor_tensor(o_sb[:, u, :], a1[:, 0:256], rden[:, 2 * u + 1:2 * u + 2],
                                                                            o_sb[:, u, :], ALU.mult, ALU.add),
                      reads=[("acc", 1, u), ("rden", u, 1), ("o_sb", u)], writes=[("o_sb", u)])
                P.add("dve", lambda e, u=u: e.tensor_tensor(osq[:, u, :], o_sb[:, u, :], o_sb[:, u, :], ALU.mult),
                      reads=[("o_sb", u)], writes=[("osq", u)])
                P.add("dve", lambda e, u=u: e.reduce_sum(ssq[:, u:u + 1], osq[:, u, :], axis=AX.X),
                      reads=[("osq", u)], writes=[("ssq", u)])
                P.add("act", lambda e, u=u: e.activation(ssq[:, u:u + 1], ssq[:, u:u + 1], AF.Sqrt, bias=float(EPS), scale=1.0 / 256),
                      reads=[("ssq", u)], writes=[("ssq", u)])
                P.add("dve", lambda e, u=u: e.reciprocal(ssq[:, u:u + 1], ssq[:, u:u + 1]), reads=[("ssq", u)], writes=[("ssq", u)])
                P.add("dve", lambda e, u=u: e.scalar_tensor_tensor(o_out[:, u, :], o_sb[:, u, :], ssq[:, u:u + 1], gvec,
                                                                   ALU.mult, ALU.mult),
                      reads=[("o_sb", u), ("ssq", u), "gvec"], writes=[("o_out", u)])
                r0 = j * QT + u * 128
                P.add("sp", lambda e, u=u, r0=r0, hh=hh: [e.dma_start(
                    out=O_heads[r0 // 1024][r0 % 1024:r0 % 1024 + 128, hh * 256:(hh + 1) * 256], in_=o_out[:, u, :])],
                      reads=[("o_out", u)], writes=[("Ohd", hh, j, u)], dma_key=("k_oo", u))


        for idx, (j, i) in enumerate(units):
            emit_scores(idx, j, i)
            if idx >= LA:
                emit_pv(idx - LA, *units[idx - LA])
        for idx in range(max(0, len(units) - LA), len(units)):
            emit_pv(idx, *units[idx])

def da3_setup(B, sel_b):
    P = B.P
    B.sel = B.alloc([128, 1], F32)
    B.I1f = B.alloc([128, 128], F32)
    B.I0b = B.alloc([128, 128], BF16)
    B.I1b = B.alloc([128, 128], BF16)
    P.add("sp", lambda e: [e.dma_start(out=B.sel, in_=sel_b[:, :])], writes=["sel"], dma_key="k_sel")
    P.add("dve", lambda e: e.tensor_scalar(B.I1f, B.ident_f, B.sel[:, 0:1], None, ALU.mult), reads=["sel", "ident_f"], writes=["I1f"])
    P.add("dve", lambda e: e.tensor_copy(B.I1b, B.I1f), reads=["I1f"], writes=["I1b"])
    P.add("dve", lambda e: e.tensor_tensor(B.I0b, B.ident_f, B.I1f, ALU.subtract), reads=["I1f", "ident_f"], writes=["I0b"])


def da3_load(B, xT0, O_all, da_w_o, t, slot):
    P = B.P
    ps = B.ps
    B.load_x_scratch(xT0, t, slot)
    ob = B.view(B.actT_off, [128, 4, 4, 1024], BF16)
    for q in range(4):
        rho, half = q // 2, q % 2
        base = rho * 1024 + (t % 2) * T
        src = O_all[2 * half + t // 2][base:base + T, :].rearrange("(u p) e -> p u e", p=128)
        P.add("sp", lambda e, q=q, src=src: [e.dma_start(out=ob[:, q], in_=src)],
              writes=[("actT", f) for f in range(8 * q, 8 * q + 8)], dma_key=("k_ob", q))
    for et in range(NKT):
        rho, el = et // 8, et % 8
        bank = 4 + et % 2

        def tr(e, rho=rho, el=el, bank=bank):
            r = None
            for u in range(4):
                o_ = ps[bank][:, u * 128:(u + 1) * 128]
                e.matmul(o_, lhsT=ob[:, 2 * rho, u, el * 128:(el + 1) * 128], rhs=B.I0b, start=True, stop=False)
                r = e.matmul(o_, lhsT=ob[:, 2 * rho + 1, u, el * 128:(el + 1) * 128], rhs=B.I1b, start=False, stop=True)
            return r
        P.add("pe", tr, reads=[("actT", f) for f in range(16 * rho, 16 * rho + 16)] + ["I0b", "I1b"], writes=[("ps", bank)])
        if et % 2 == 0:
            P.add("act", lambda e, et=et, bank=bank: e.activation(B.hT[:, et, :], ps[bank][:, :], AF.Identity),
                  reads=[("ps", bank)], writes=[("hT", et)])
        else:
            P.add("dve", lambda e, et=et, bank=bank: e.tensor_copy(B.hT[:, et, :], ps[bank][:, :]),
                  reads=[("ps", bank)], writes=[("hT", et)])
    mixer_out_proj(B, da_w_o, NKT, B.hT, [("hT", k) for k in range(NKT)], slot)


def mixer_out_proj(B, w_o, n_et, OT, OT_res, slot):
    P = B.P
    ps = B.ps
    G2 = B.modv[:, 3 * 1 + 2, :]
    xt = B.xt[slot]
    wsrc = w_o.rearrange("(et p) d -> p et d", p=128)
    for gd in range(NKT // 2):
        s = B.next_w()
        wv = B.wview(s, [128, n_et, 256])
        P.add("pool", lambda e, wv=wv, gd=gd: [e.dma_start(out=wv, in_=wsrc[:, :, gd * 256:(gd + 1) * 256])],
              writes=[("w", s)], dma_key=("kw", s))
        for j in range(2):
            dt_ = 2 * gd + j
            py = ps[4 + dt_ % 2]

            def mmo(e, wv=wv, j=j, py=py):
                r = None
                for et in range(n_et):
                    r = e.matmul(py[:, :], lhsT=wv[:, et, j * 128:(j + 1) * 128], rhs=OT[:, et, :],
                                 start=(et == 0), stop=(et == n_et - 1))
                return r
            P.add("pe", mmo, reads=[("w", s)] + OT_res, writes=[("ps", 4 + dt_ % 2)])
            P.add("dve", lambda e, py=py, dt_=dt_: e.scalar_tensor_tensor(
                xt[:, dt_, :], py[:, :], G2[:, dt_:dt_ + 1], xt[:, dt_, :], ALU.mult, ALU.add),
                reads=[("ps", 4 + dt_ % 2), ("modv", 1, 2), ("xt", slot, dt_)], writes=[("xt", slot, dt_)])


def ret_part1(B, xT_d, ret_w, QTd, KTd, Ktd, Vd, Gd):
    P = B.P
    ps = B.ps
    wsrc = ret_w.rearrange("(kt p) n -> p kt n", p=128)
    stg = B.view(B.actT_off, [128, 2, 4, 512], BF16)
    kscale = 256 ** -0.5
    cnt = 0
    for t in range(NT):
        slot = t % 2
        B.load_x_scratch(xT_d, t, slot)
        B.norm_tile(slot, 1)
        hres = [("hT", k) for k in range(NKT)]
        for g in range(24):
            s = B.next_w()
            wv = B.wview(s, [128, NKT, 512])
            P.add("pool", lambda e, wv=wv, g=g: [e.dma_start(out=wv, in_=wsrc[:, :, g * 512:(g + 1) * 512])],
                  writes=[("w", s)], dma_key=("kw", s))
            if g < 8:
                ss = cnt % 2
                cnt += 1
                for j in range(4):
                    bank = j % 2

                    def mmf(e, wv=wv, j=j, bank=bank):
                        r = None
                        for kt in range(NKT):
                            r = e.matmul(ps[bank][:, :], lhsT=wv[:, kt, j * 128:(j + 1) * 128], rhs=B.hT[:, kt, :],
                                         start=(kt == 0), stop=(kt == NKT - 1))
                        return r
                    P.add("pe", mmf, reads=[("w", s)] + hres, writes=[("ps", bank)])
                    sc = 1.0 if g < 4 else kscale
                    P.add("act", lambda e, ss=ss, j=j, bank=bank, sc=sc: e.activation(
                        stg[:, ss, j, :], ps[bank][:, :], AF.Identity, scale=sc),
                        reads=[("ps", bank)], writes=[("actT", ss * 4 + j)])
                dstT = (QTd if g < 4 else KTd)
                r0 = (g % 4) * 512
                dst = dstT[r0:r0 + 512, t * T:(t + 1) * T].rearrange("(j p) t -> p j t", p=128)
                P.add("sp", lambda e, dst=dst, ss=ss: [e.dma_start(out=dst, in_=stg[:, ss])],
                      reads=[("actT", ss * 4 + j) for j in range(4)], writes=[("retTd", g, t)], dma_key=("k_rs", ss))
            if g >= 4:
                ss = cnt % 2
                cnt += 1
                for u in range(4):
                    bank = 2 + u % 2

                    def mmt(e, wv=wv, u=u, bank=bank):
                        r = None
                        for kt in range(NKT):
                            r = e.matmul(ps[bank][:, :], lhsT=B.hT[:, kt, u * 128:(u + 1) * 128], rhs=wv[:, kt, :],
                                         start=(kt == 0), stop=(kt == NKT - 1))
                        return r
                    P.add("pe", mmt, reads=[("w", s)] + hres, writes=[("ps", bank)])
                    if g < 8:
                        P.add("act", lambda e, ss=ss, u=u, bank=bank: e.activation(
                            stg[:, ss, u, :], ps[bank][:, :], AF.Identity, scale=kscale),
                            reads=[("ps", bank)], writes=[("actT", ss * 4 + u)])
                    elif g < 16:
                        P.add("dve", lambda e, ss=ss, u=u, bank=bank: e.tensor_copy(stg[:, ss, u, :], ps[bank][:, :]),
                              reads=[("ps", bank)], writes=[("actT", ss * 4 + u)])
                    else:
                        P.add("act", lambda e, ss=ss, u=u, bank=bank: e.activation(
                            stg[:, ss, u, :], ps[bank][:, :], AF.Silu),
                            reads=[("ps", bank)], writes=[("actT", ss * 4 + u)])
                if g < 8:
                    dd, c0 = Ktd, (g - 4) * 512
                elif g < 16:
                    dd, c0 = Vd, (g - 8) * 512
                else:
                    dd, c0 = Gd, (g - 16) * 512
                dst = dd[t * T:(t + 1) * T, c0:c0 + 512].rearrange("(u p) c -> p u c", p=128)
                P.add("sp", lambda e, dst=dst, ss=ss: [e.dma_start(out=dst, in_=stg[:, ss])],
                      reads=[("actT", ss * 4 + u) for u in range(4)], writes=[("retd", g, t)], dma_key=("k_rs", ss))


def ret_part2(B, Ktd, Vd, kdec_glob, S_loc):
    P = B.P
    ps = B.ps
    off = B.phase_base

    def al(shape, dt):
        nonlocal off
        esz = 4 if dt == F32 else 2
        n = 1
        for s in shape[1:]:
            n *= s
        v = B.view(off, shape, dt)
        off += (n * esz + 63) // 64 * 64
        assert off <= B.arena_bytes, off
        return v
    kb = [al([128, 16, 256], BF16) for _ in range(2)]
    kd = [al([128, 16, 256], BF16) for _ in range(2)]
    vb = [al([128, 16, 512], BF16) for _ in range(2)]
    kdec = al([128, 8, 16], F32)
    sst = [al([128, 2, 512], F32) for _ in range(2)]
    P.add("sp", lambda e: [e.dma_start(out=kdec, in_=kdec_glob[:, :, :])], writes=["kdecg"], dma_key="k_kdg")
    for h in range(RET_H):
        b_ = h % 2
        ksrc = Ktd[:, h * 256:(h + 1) * 256].rearrange("(st p) c -> p st c", p=128)
        vsrc = Vd[:, h * 512:(h + 1) * 512].rearrange("(st p) c -> p st c", p=128)
        P.add("sp", lambda e, b_=b_, ksrc=ksrc, vsrc=vsrc: [e.dma_start(out=kb[b_], in_=ksrc), e.dma_start(out=vb[b_], in_=vsrc)],
              reads=["retd_all"], writes=[("kb", b_), ("vb", b_)], dma_key=("k_r2", b_), ndma=2)
        for st in range(16):
            P.add("dve", lambda e, b_=b_, st=st, h=h: e.tensor_scalar(kd[b_][:, st, :], kb[b_][:, st, :],
                                                                        kdec[:, h, st:st + 1], None, ALU.mult),
                  reads=[("kb", b_), "kdecg"], writes=[("kd", b_, st)])
        for dkt in range(2):
            bank = dkt

            def mms(e, b_=b_, dkt=dkt, bank=bank):
                r = None
                for st in range(16):
                    r = e.matmul(ps[bank][:, :], lhsT=kd[b_][:, st, dkt * 128:(dkt + 1) * 128], rhs=vb[b_][:, st, :],
                                 start=(st == 0), stop=(st == 15))
                return r
            P.add("pe", mms, reads=[("kd", b_, st) for st in range(16)] + [("vb", b_)], writes=[("ps", bank)])
            if dkt == 0:
                P.add("act", lambda e, b_=b_, dkt=dkt, bank=bank: e.activation(sst[b_][:, dkt, :], ps[bank][:, :], AF.Identity),
                      reads=[("ps", bank)], writes=[("sst", b_, dkt)])
            else:
                P.add("dve", lambda e, b_=b_, dkt=dkt, bank=bank: e.tensor_copy(sst[b_][:, dkt, :], ps[bank][:, :]),
                      reads=[("ps", bank)], writes=[("sst", b_, dkt)])
        dst = S_loc[h // 4][(h % 4) * 256:(h % 4 + 1) * 256, :].rearrange("(a p) c -> p a c", p=128)
        P.add("sp", lambda e, b_=b_, dst=dst: [e.dma_start(out=dst, in_=sst[b_])],
              reads=[("sst", b_, 0), ("sst", b_, 1)], writes=[("Slocd", h)], dma_key=("k_r2s", b_))


def ret_part3(B, QTd, KTd, Ktd, Vd, Gd, S_all, sel_b, ret_dmask, ret_qdec, ret_kdec, ret_gng_b, OGd):
    P = B.P
    ps = B.ps
    off = B.phase_base

    def al(shape, dt):
        nonlocal off
        esz = 4 if dt == F32 else 2
        n = 1
        for s in shape[1:]:
            n *= s
        v = B.view(off, shape, dt)
        off += (n * esz + 63) // 64 * 64
        assert off <= B.arena_bytes, off
        return v
    qn = [al([128, 16, 128], BF16) for _ in range(2)]
    kn = [al([128, 16, 128], BF16) for _ in range(2)]
    ktn = [al([128, 2048], BF16) for _ in range(2)]
    vn = [al([128, 4096], BF16) for _ in range(2)]
    gn = [al([128, 4096], BF16) for _ in range(2)]
    ogn = [al([128, 4096], BF16) for _ in range(2)]
    S_f = al([128, 16, 512], F32)
    S_b = al([128, 16, 512], BF16)
    dmask = al([128, 8, 128], F32)
    qdec = al([128, 8], F32)
    kdec = al([128, 8], F32)
    gng = al([128, 512], F32)
    sel = al([128, 1], F32)
    at_sb = al([128, 2, 128], BF16)
    kd_sb = al([128, 2, 256], BF16)
    oi_sb = al([128, 2, 512], F32)
    o_sb = al([128, 2, 512], F32)
    osq = al([128, 2, 512], F32)
    y_sb = al([128, 2, 512], F32)
    st = al([128, 2, 8], F32)
    P.add("sp", lambda e: [e.dma_start(out=dmask, in_=ret_dmask[:, :, :]), e.dma_start(out=qdec, in_=ret_qdec[:, :]),
                           e.dma_start(out=kdec, in_=ret_kdec[:, :]), e.dma_start(out=gng, in_=ret_gng_b[:, :]),
                           e.dma_start(out=sel, in_=sel_b[:, :]),
                           e.dma_start(out=S_f[:, 0:8, :], in_=S_all[0][0:1024, :].rearrange("(c p) v -> p c v", p=128)),
                           e.dma_start(out=S_f[:, 8:16, :], in_=S_all[1][0:1024, :].rearrange("(c p) v -> p c v", p=128))],
          writes=["r3c"] + [("S_f", c) for c in range(16)], dma_key="k_r3c", ndma=7)
    for c in range(16):
        P.add("dve" if c % 2 == 0 else POOLC, lambda e, c=c: e.tensor_scalar(S_f[:, c, :], S_f[:, c, :], sel[:, 0:1], None, ALU.mult),
              reads=["r3c", ("S_f", c)], writes=[("S_f", c)])
        P.add("act", lambda e, c=c: e.activation(S_b[:, c, :], S_f[:, c, :], AF.Identity), reads=[("S_f", c)], writes=[("S_b", c)])
    NSC = DEBUG.get('r3n', 16)

    def r3_load(n):
        b_ = n % 2
        t0_ = n * 128
        P.add("sp", lambda e, b_=b_, t0_=t0_: [
            e.dma_start(out=qn[b_], in_=QTd.ap().rearrange("(c p) t -> p c t", p=128)[:, :, t0_:t0_ + 128]),
            e.dma_start(out=kn[b_], in_=KTd.ap().rearrange("(c p) t -> p c t", p=128)[:, :, t0_:t0_ + 128]),
            e.dma_start(out=ktn[b_], in_=Ktd[t0_:t0_ + 128, :]),
            e.dma_start(out=vn[b_], in_=Vd[t0_:t0_ + 128, :]),
            e.dma_start(out=gn[b_], in_=Gd[t0_:t0_ + 128, :])],
            writes=[("r3in", b_)], dma_key=("k_r3l", b_), ndma=5)

    r3_load(0)
    for n in range(NSC):
        b_ = n % 2
        t0_ = n * 128
        if n + 1 < NSC:
            r3_load(n + 1)
        for h in range(DEBUG.get('r3h', RET_H)):
            p_ = h % 2
            gam = 1.0 - 2.0 ** (-5.0 - h)
            g128 = float(gam ** 128)
            psA = ps[0][:, p_ * 128:(p_ + 1) * 128]
            p_oi = ps[1 + p_]
            p_oc = ps[3 + p_]

            def mmA(e, b_=b_, h=h, psA=psA):
                r = None
                for dkt in range(2):
                    r = e.matmul(psA, lhsT=kn[b_][:, 2 * h + dkt, :], rhs=qn[b_][:, 2 * h + dkt, :],
                                 start=(dkt == 0), stop=(dkt == 1))
                return r
            P.add("pe", mmA, reads=[("r3in", b_)], writes=[("psA", p_)])
            P.add("dve", lambda e, p_=p_, h=h, psA=psA: e.tensor_tensor(at_sb[:, p_, :], psA, dmask[:, h, :], ALU.mult),
                  reads=[("psA", p_), "r3c"], writes=[("at", p_)])
            P.add("pe", lambda e, p_=p_, b_=b_, h=h, p_oi=p_oi: e.matmul(p_oi[:, :], lhsT=at_sb[:, p_, :],
                                                                         rhs=vn[b_][:, h * 512:(h + 1) * 512], start=True, stop=True),
                  reads=[("at", p_), ("r3in", b_)], writes=[("ps", 1 + p_)])

            def mmC(e, b_=b_, h=h, p_oc=p_oc):
                r = None
                for dkt in range(2):
                    r = e.matmul(p_oc[:, :], lhsT=qn[b_][:, 2 * h + dkt, :], rhs=S_b[:, 2 * h + dkt, :],
                                 start=(dkt == 0), stop=(dkt == 1))
                return r
            P.add("pe", mmC, reads=[("r3in", b_), ("S_b", 2 * h), ("S_b", 2 * h + 1)], writes=[("ps", 3 + p_)])
            P.add("act", lambda e, p_=p_, p_oi=p_oi: e.activation(oi_sb[:, p_, :], p_oi[:, :], AF.Identity),
                  reads=[("ps", 1 + p_)], writes=[("oi", p_)])
            P.add("dve", lambda e, p_=p_, h=h, p_oc=p_oc: e.scalar_tensor_tensor(o_sb[:, p_, :], p_oc[:, :], qdec[:, h:h + 1],
                                                                                oi_sb[:, p_, :], ALU.mult, ALU.add),
                  reads=[("ps", 3 + p_), ("oi", p_), "r3c"], writes=[("o", p_)])
            if 'state' in DEBUG.get('r3skip', ()):
                continue
            P.add(POOLC, lambda e, p_=p_, b_=b_, h=h: e.tensor_scalar(kd_sb[:, p_, :], ktn[b_][:, h * 256:(h + 1) * 256],
                                                                       kdec[:, h:h + 1], None, ALU.mult),
                  reads=[("r3in", b_), "r3c"], writes=[("kd", p_)])
            for dkt in range(2):
                bank = 5 + dkt
                c = 2 * h + dkt
                P.add("pe", lambda e, p_=p_, b_=b_, h=h, dkt=dkt, bank=bank: e.matmul(
                    ps[bank][:, :], lhsT=kd_sb[:, p_, dkt * 128:(dkt + 1) * 128], rhs=vn[b_][:, h * 512:(h + 1) * 512],
                    start=True, stop=True),
                    reads=[("kd", p_), ("r3in", b_)], writes=[("ps", bank)])
                P.add("dve", lambda e, c=c, bank=bank, g128=g128: e.scalar_tensor_tensor(
                    S_f[:, c, :], S_f[:, c, :], g128, ps[bank][:, :], ALU.mult, ALU.add),
                    reads=[("ps", bank), ("S_f", c)], writes=[("S_f", c)])
                P.add("act", lambda e, c=c: e.activation(S_b[:, c, :], S_f[:, c, :], AF.Identity),
                      reads=[("S_f", c)], writes=[("S_b", c)])
            if 'ln' in DEBUG.get('r3skip', ()):
                continue
            sv = st[:, p_, :]
            P.add("dve", lambda e, p_=p_, sv=sv: e.reduce_sum(sv[:, 0:1], o_sb[:, p_, :], axis=AX.X),
                  reads=[("o", p_)], writes=[("st", p_, 0)])
            P.add(POOLC, lambda e, p_=p_: e.tensor_tensor(osq[:, p_, :], o_sb[:, p_, :], o_sb[:, p_, :], ALU.mult),
                  reads=[("o", p_)], writes=[("osq", p_)])
            P.add("dve", lambda e, p_=p_, sv=sv: e.reduce_sum(sv[:, 1:2], osq[:, p_, :], axis=AX.X),
                  reads=[("osq", p_)], writes=[("st", p_, 1)])
            P.add("dve", lambda e, sv=sv: e.tensor_scalar(sv[:, 2:3], sv[:, 0:1], -1.0 / 512, None, ALU.mult),
                  reads=[("st", p_, 0)], writes=[("st", p_, 2)])
            P.add("dve", lambda e, sv=sv: e.tensor_tensor(sv[:, 3:4], sv[:, 2:3], sv[:, 2:3], ALU.mult),
                  reads=[("st", p_, 2)], writes=[("st", p_, 3)])
            P.add("dve", lambda e, sv=sv: e.scalar_tensor_tensor(sv[:, 4:5], sv[:, 1:2], 1.0 / 512, sv[:, 3:4], ALU.mult, ALU.subtract),
                  reads=[("st", p_, 1), ("st", p_, 3)], writes=[("st", p_, 4)])
            P.add("act", lambda e, sv=sv: e.activation(sv[:, 4:5], sv[:, 4:5], AF.Sqrt, bias=float(EPS), scale=1.0),
                  reads=[("st", p_, 4)], writes=[("st", p_, 4)])
            P.add("dve", lambda e, sv=sv: e.reciprocal(sv[:, 4:5], sv[:, 4:5]), reads=[("st", p_, 4)], writes=[("st", p_, 4)])
            P.add("dve", lambda e, p_=p_, sv=sv: e.tensor_scalar(y_sb[:, p_, :], o_sb[:, p_, :], sv[:, 2:3], sv[:, 4:5], ALU.add, ALU.mult),
                  reads=[("o", p_), ("st", p_, 2), ("st", p_, 4)], writes=[("y", p_)])
            P.add(POOLC, lambda e, p_=p_: e.tensor_tensor(y_sb[:, p_, :], y_sb[:, p_, :], gng, ALU.mult),
                  reads=[("y", p_), "r3c"], writes=[("y", p_)])
            P.add(POOLC, lambda e, p_=p_, b_=b_, h=h: e.tensor_tensor(ogn[b_][:, h * 512:(h + 1) * 512], y_sb[:, p_, :],
                                                                       gn[b_][:, h * 512:(h + 1) * 512], ALU.mult),
                  reads=[("y", p_), ("r3in", b_)], writes=[("ogn", b_, h)])
        P.add("sp", lambda e, b_=b_, t0_=t0_: [e.dma_start(out=OGd[t0_:t0_ + 128, :], in_=ogn[b_])],
              reads=[("ogn", b_, h) for h in range(DEBUG.get('r3h', RET_H))], writes=[("OGd", n)], dma_key=("k_r3s", b_))


def ret4_load(B, xT_d, OGd, ret_w_o, t, slot):
    P = B.P
    ps = B.ps
    B.load_x_scratch(xT_d, t, slot)
    OGT = B.view(B.actT_off, [128, 32, T], BF16)
    stg = B.view(B.actT_off + 32 * T * 2, [128, 4096], BF16)
    stg_res = [("actT", f) for f in range(32, 40)]
    for u in range(4):
        r0 = t * T + u * 128
        P.add("sp", lambda e, r0=r0: [e.dma_start(out=stg, in_=OGd[r0:r0 + 128, :])], writes=stg_res, dma_key="k_r4l")
        for g in range(8):
            bank = 6 + g % 2
            psb = ps[bank][:, 0:256].bitcast(BF16)

            def tr(e, g=g, psb=psb):
                r = None
                for j in range(4):
                    et = g * 4 + j
                    r = e.transpose(psb[:, j * 128:(j + 1) * 128], stg[:, et * 128:(et + 1) * 128], B.ident_b)
                return r
            P.add("pe", tr, reads=stg_res + ["ident_b"], writes=[("ps", bank)])
            dst = OGT[:, g * 4:(g + 1) * 4, u * 128:(u + 1) * 128]
            srcp = psb.rearrange("p (a b) -> p a b", a=4)
            if g % 2 == 0:
                P.add("dve", lambda e, dst=dst, srcp=srcp: e.tensor_copy(dst, srcp), reads=[("ps", bank)],
                      writes=[("actT", g * 4 + j) for j in range(4)])
            else:
                P.add("act", lambda e, dst=dst, srcp=srcp: e.activation(dst, srcp, AF.Identity), reads=[("ps", bank)],
                      writes=[("actT", g * 4 + j) for j in range(4)])
    mixer_out_proj(B, ret_w_o, 32, OGT, [("actT", f) for f in range(32)], slot)


def final_store(B, out_d, fg_pm_sb, t, slot):
    P = B.P
    ps = B.ps
    xt = B.xt[slot]
    pst = ps[6]
    for kt in range(NKT):
        q = kt % 4
        P.add("act", lambda e, kt=kt, q=q: e.activation(B.sq[:, q, :], xt[:, kt, :], AF.Square),
              reads=[("xt", slot, kt)], writes=[("sq", q)])
        P.add("pe", lambda e, kt=kt, q=q: e.matmul(pst[:, :], lhsT=B.ones_b, rhs=B.sq[:, q, :],
                                                   start=(kt == 0), stop=(kt == NKT - 1)),
              reads=[("sq", q), "ones_b"], writes=[("ps", 6)])
    P.add("act", lambda e: e.activation(B.rstd, pst[:, :], AF.Sqrt, bias=float(D * EPS), scale=1.0),
          reads=[("ps", 6)], writes=["rstd"])
    P.add("dve", lambda e: e.reciprocal(B.rstd, B.rstd), reads=["rstd"], writes=["rstd"])
    for kt in range(NKT):
        q = kt % 2
        P.add("dve", lambda e, kt=kt, q=q: e.tensor_tensor(B.tmp[:, q, :], xt[:, kt, :], B.rstd, ALU.mult),
              reads=[("xt", slot, kt), "rstd"], writes=[("tmp", q)])
        P.add("act", lambda e, kt=kt, q=q: e.activation(xt[:, kt, :], B.tmp[:, q, :], AF.Identity, scale=fg_pm_sb[:, kt:kt + 1]),
              reads=[("tmp", q), "fg"], writes=[("xt", slot, kt)])
    stg = B.view(B.actT_off + 32 * T * 2, [128, D], F32)
    stg_res = [("actT", f) for f in range(32, 40)]
    for u in range(4):
        for g in range(4):
            bank = 4 + g % 2

            def tr(e, g=g, u=u, bank=bank):
                r = None
                for j in range(4):
                    kt = g * 4 + j
                    r = e.transpose(ps[bank][:, j * 128:(j + 1) * 128], xt[:, kt, u * 128:(u + 1) * 128], B.ident_f)
                return r
            P.add("pe", tr, reads=[("xt", slot, g * 4 + j) for j in range(4)] + ["ident_f"], writes=[("ps", bank)])
            if g % 2 == 0:
                P.add("dve", lambda e, g=g, bank=bank: e.tensor_copy(stg[:, g * 512:(g + 1) * 512], ps[bank][:, :]),
                      reads=[("ps", bank)], writes=[("actT", 32 + 2 * g), ("actT", 33 + 2 * g)])
            else:
                P.add("act", lambda e, g=g, bank=bank: e.activation(stg[:, g * 512:(g + 1) * 512], ps[bank][:, :], AF.Identity),
                      reads=[("ps", bank)], writes=[("actT", 32 + 2 * g), ("actT", 33 + 2 * g)])
        r0 = t * T + u * 128
        P.add("sp", lambda e, r0=r0: [e.dma_start(out=out_d[r0:r0 + 128, :], in_=stg)],
              reads=stg_res, writes=[("outd", t, u)], dma_key="k_out")


def _pm(v):
    return np.ascontiguousarray(np.asarray(v, np.float32).reshape(16, 128).T)


def _da_consts(r):
    bias = np.zeros((4, 3, 128, QT), np.float32)
    offs = np.zeros((128, 128), np.float32)
    kr = np.arange(128)[:, None]
    qr = np.arange(QT)[None, :]
    for hh in range(4):
        h = 4 * r + hh
        slope = 2.0 ** (-8.0 * (h + 1) / 8)
        bias[hh, 0] = -slope * (qr - kr)
        for d in range(2):
            allowed = ((128 * d + kr) // 64) <= (qr // 64)
            bias[hh, 1 + d] = np.where(allowed, -slope * np.abs(qr - kr - 128 * d), NEG)
        for n in range(32):
            offs[:, hh * 32 + n] = -slope * 128 * n
    return bias, offs


def _ret_consts():
    gam = np.array([1.0 - 2.0 ** (-5.0 - h) for h in range(RET_H)], np.float64)
    p = np.arange(128)
    k = p[:, None, None]
    q = p[None, None, :]
    dm = gam[None, :, None] ** np.abs(q - k) * ((k // 64) <= (q // 64))
    qdec = gam[None, :] ** p[:, None]
    kdec = gam[None, :] ** (128 - p[:, None])
    st = np.arange(16)
    kg = gam[None, :, None] ** (TOK - (st[None, None, :] * 128 + p[:, None, None]))
    return (dm.astype(np.float32), qdec.astype(np.float32), kdec.astype(np.float32), kg.astype(np.float32))


_PROGS = {}
CC_UNIT = 1
POOLC = 'dve'
DEBUG = {}


def _prog(stages):
    key = tuple(stages)
    if key not in _PROGS:
        _PROGS[key] = build_program(list(stages))
    return _PROGS[key]


def _core_maps(inputs):
    f = lambda a: np.ascontiguousarray(np.asarray(a, np.float32))
    x = np.asarray(inputs["x"], np.float32)
    dm, qdec, kdec, kg = _ret_consts()
    ada_b_pm = np.ascontiguousarray(np.asarray(inputs["ada_b"], np.float32).reshape(2, 144, 128).transpose(0, 2, 1))
    norm_g_pm = np.ascontiguousarray(np.asarray(inputs["norm_g"], np.float32).reshape(2, 3, 16, 128).transpose(0, 3, 1, 2))
    w = np.asarray(inputs["da_w_qkv"], np.float32)[0]
    shared = {
        "c_ident": np.eye(128, dtype=np.float32),
        "ada_w_0": f(inputs["ada_w"][0]), "ada_w_1": f(inputs["ada_w"][1]),
        "ada_b_pm": ada_b_pm, "norm_g_pm": norm_g_pm,
        "ffn_w_in_00": f(inputs["ffn_w_in"][0, 0]), "ffn_w_in_01": f(inputs["ffn_w_in"][0, 1]),
        "ffn_w_in_10": f(inputs["ffn_w_in"][1, 0]), "ffn_w_in_11": f(inputs["ffn_w_in"][1, 1]),
        "ffn_w_out_00": f(inputs["ffn_w_out"][0, 0]), "ffn_w_out_01": f(inputs["ffn_w_out"][0, 1]),
        "ffn_w_out_10": f(inputs["ffn_w_out"][1, 0]), "ffn_w_out_11": f(inputs["ffn_w_out"][1, 1]),
        "da_lam_b": np.ascontiguousarray(np.broadcast_to(np.asarray(inputs["da_lambda"], np.float32)[0].reshape(1, 512), (128, 512))),
        "da_gvec_b": np.ascontiguousarray(np.broadcast_to(np.asarray(inputs["da_subln_g"], np.float32)[0][None], (128, 256))),
        "da_w_o": f(inputs["da_w_o"][0]),
        "ret_w_qkvg": f(inputs["ret_w_qkvg"][0]),
        "ret_kdec_glob": kg, "ret_dmask": dm, "ret_qdec": qdec, "ret_kdec": kdec,
        "ret_gng_b": np.ascontiguousarray(np.broadcast_to(np.asarray(inputs["ret_gn_g"], np.float32)[0][None], (128, 512))),
        "ret_w_o": f(inputs["ret_w_o"][0]),
        "final_g_pm": _pm(inputs["final_g"]),
    }
    da_my = []
    for r in range(2):
        hs = range(4 * r, 4 * r + 4)
        wq = np.concatenate([w[:, h * 256:(h + 1) * 256] for h in hs], 1)
        wk = np.concatenate([w[:, 2048 + h * 256:2048 + (h + 1) * 256] for h in hs], 1)
        wv = np.concatenate([w[:, 4096 + h * 256:4096 + (h + 1) * 256] for h in hs], 1)
        bias, offs = _da_consts(r)
        da_my.append({"da_wqkv_my": np.ascontiguousarray(np.concatenate([wq, wk, wv], 1)), "da_bias": bias, "da_offs": offs,
                      "sel_b": np.full((128, 1), float(r), np.float32)})
    maps = []
    for c in range(NCORES):
        b, r = c // 2, c % 2
        m = dict(shared)
        m.update(da_my[r])
        m["c_pm"] = _pm(inputs["c"][b])
        m["x_in"] = np.ascontiguousarray(x[b, r * TOK:(r + 1) * TOK])
        maps.append(m)
    return maps


def _input_names(nc):
    names = []
    for alloc in nc.allocations:
        if isinstance(alloc, mybir.MemoryLocationSet) and alloc.kind == "ExternalInput":
            names.append(alloc.memorylocations[0].name)
    return names


def _run(stages, maps, cores=None):
    nc = _prog(stages)
    names = set(_input_names(nc))
    cores = list(range(NCORES)) if cores is None else cores
    in_maps = [{k: v for k, v in maps[c].items() if k in names} for c in cores]
    res = run_bass_kernel_spmd(nc, in_maps, core_ids=list(range(len(cores))))
    return res.results


FUSED = True


def kernel(**inputs):
    maps = _core_maps(inputs)
    if not FUSED:
        r0 = _run([0], maps)
        for c in range(NCORES):
            b = c // 2
            maps[c]["xT0"] = r0[c]["xT0"]
            for t in range(NT):
                maps[c]["hT_all_%d" % t] = np.concatenate([r0[2 * b]["hT_own_%d" % t], r0[2 * b + 1]["hT_own_%d" % t]], 0)
        r1 = _run([1], maps)
        for c in range(NCORES):
            b = c // 2
            for k in range(4):
                maps[c]["O_all_%d" % k] = np.concatenate([r1[2 * b]["O_heads_%d" % k], r1[2 * b + 1]["O_heads_%d" % k]], 0)
        r2 = _run([2], maps)
        for c in range(NCORES):
            b = c // 2
            for k in ("xTb", "ret_QT", "ret_KT", "ret_Kt", "ret_V", "ret_G"):
                maps[c][k] = r2[c][k]
            for k in range(2):
                maps[c]["S_all_%d" % k] = np.concatenate([r2[2 * b]["S_loc_%d" % k], r2[2 * b + 1]["S_loc_%d" % k]], 0)
        r3 = _run([3], maps)
        outs = r3
    else:
        outs = _run([0, 1, 2, 3], maps)
    out = np.zeros((4, SEQ, D), np.float32)
    for c in range(NCORES):
        b, r = c // 2, c % 2
        out[b, r * TOK:(r + 1) * TOK] = outs[c]["out"]
    return out
```
